# Optimizing a Trainium2 kernel written in Bass

```python
import math
import jax, jax.numpy as jnp
from jax import lax
import numpy as np

D_MODEL = 2048
BATCH = 16
SEQ = 2048
DEPTH = 1

LRU_W = D_MODEL // 2
LRU_BLOCKS = 8
LRU_BW = LRU_W // LRU_BLOCKS
CONV_W = 4
LRU_C = 8.0
ATT_DV = 128
ATT_DQK = ATT_DV // 2
ATT_HEADS = (D_MODEL - LRU_W) // ATT_DV
ATT_QK = 2 * ATT_HEADS * ATT_DQK
ATT_V = ATT_HEADS * ATT_DV
MIX_W = LRU_W + ATT_V
IN_W = 2 * LRU_W + 2 * ATT_QK + ATT_V
Q_BLOCK = 128
REL_BUCKETS = 32
REL_MAX_DIST = 128
FFN_HIDDEN = int(math.ceil(8 * D_MODEL / 3 / 256)) * 256
PLE_DIM = 256
LN_EPS = 1e-5
ALPHA = (2 * DEPTH) ** 0.25
BETA = (8 * DEPTH) ** -0.25

kernel_name = "hymba_rglru_diffattn_deepnorm_ple"


def layer_norm(x, g, b):
    xf = x.astype(jnp.float32)
    mu = jnp.mean(xf, axis=-1, keepdims=True)
    xc = xf - mu
    var = jnp.mean(xc * xc, axis=-1, keepdims=True)
    return (xc * lax.rsqrt(var + LN_EPS) * g.astype(jnp.float32) + b.astype(jnp.float32)).astype(x.dtype)


def rms_norm(x, g):
    xf = x.astype(jnp.float32)
    y = xf * lax.rsqrt(jnp.mean(xf * xf, axis=-1, keepdims=True) + LN_EPS)
    return y * g.astype(jnp.float32)


def t5_causal_bucket(rel):
    n = jnp.maximum(-rel, 0)
    max_exact = REL_BUCKETS // 2
    nf = jnp.maximum(n, 1).astype(jnp.float32)
    large = max_exact + (jnp.log(nf / max_exact) / math.log(REL_MAX_DIST / max_exact)
                         * (REL_BUCKETS - max_exact)).astype(jnp.int32)
    large = jnp.minimum(large, REL_BUCKETS - 1)
    return jnp.where(n < max_exact, n, large)


def causal_depthwise_conv(x, w, b):
    S = x.shape[1]
    xp = jnp.pad(x, ((0, 0), (CONV_W - 1, 0), (0, 0)))
    y = b
    for t in range(CONV_W):
        y = y + xp[:, t:t + S, :] * w[t]
    return y


def rg_lru(x, w_a, b_a, w_x, b_x, lam):
    B, S, W = x.shape
    xf = x.astype(jnp.float32)
    xb = xf.reshape(B, S, LRU_BLOCKS, LRU_BW)
    gate_x = jax.nn.sigmoid(jnp.einsum('bsnc,ncd->bsnd', xb, w_x.astype(jnp.float32)) + b_x.astype(jnp.float32)).reshape(B, S, W)
    gate_a = jax.nn.sigmoid(jnp.einsum('bsnc,ncd->bsnd', xb, w_a.astype(jnp.float32)) + b_a.astype(jnp.float32)).reshape(B, S, W)
    log_a = -LRU_C * gate_a * jax.nn.softplus(-lam.astype(jnp.float32))
    a = jnp.exp(log_a)
    mult = jnp.sqrt(-jnp.expm1(2.0 * log_a))
    u = mult * (gate_x * xf)

    def step(h, au):
        a_t, u_t = au
        h = a_t * h + u_t
        return h, h

    h0 = jnp.zeros((B, W), jnp.float32)
    _, hs = lax.scan(step, h0, (jnp.swapaxes(a, 0, 1), jnp.swapaxes(u, 0, 1)))
    return jnp.swapaxes(hs, 0, 1).astype(x.dtype)


def diff_attention(q, k, v, lq1, lk1, lq2, lk2, subln_g, rel_bias, lam_init):
    B, S, _ = q.shape
    H = ATT_HEADS
    q = q.reshape(B, S, 2 * H, ATT_DQK) * (ATT_DQK ** -0.5)
    k = k.reshape(B, S, 2 * H, ATT_DQK)
    v = v.reshape(B, S, H, ATT_DV)
    lam = (jnp.exp(jnp.sum(lq1.astype(jnp.float32) * lk1.astype(jnp.float32)))
           - jnp.exp(jnp.sum(lq2.astype(jnp.float32) * lk2.astype(jnp.float32))) + lam_init)
    nb = S // Q_BLOCK
    qb = jnp.swapaxes(q.reshape(B, nb, Q_BLOCK, 2 * H, ATT_DQK), 0, 1)
    starts = jnp.arange(nb, dtype=jnp.int32) * Q_BLOCK
    kpos = jnp.arange(S, dtype=jnp.int32)
    table = rel_bias.astype(jnp.float32)

    def one_block(args):
        q_blk, start = args
        qpos = start + jnp.arange(Q_BLOCK, dtype=jnp.int32)
        rel = kpos[None, :] - qpos[:, None]
        bias = jnp.transpose(table[t5_causal_bucket(rel)], (2, 0, 1))
        s = jnp.einsum('bqhd,bkhd->bhqk', q_blk, k).astype(jnp.float32)
        s = s.reshape(B, H, 2, Q_BLOCK, S) + bias[None, :, None]
        s = jnp.where((rel <= 0)[None, None, None], s, -jnp.inf)
        pr = jax.nn.softmax(s, axis=-1)
        attn = pr[:, :, 0] - lam * pr[:, :, 1]
        return jnp.einsum('bhqk,bkhd->bqhd', attn.astype(v.dtype), v)

    out = lax.map(one_block, (qb, starts))
    out = jnp.swapaxes(out, 0, 1).reshape(B, S, H, ATT_DV)
    out = rms_norm(out, subln_g) * (1.0 - lam_init)
    return out.reshape(B, S, ATT_V).astype(q.dtype)


def hybrid_mixer(h, w_in, conv_w, conv_b, lru_wa, lru_ba, lru_wx, lru_bx, lru_lambda,
                 lq1, lk1, lq2, lk2, subln_g, rel_bias, w_out, lam_init):
    proj = jnp.einsum('bsd,de->bse', h, w_in)
    xr, yg, q, k, v = jnp.split(
        proj, [LRU_W, 2 * LRU_W, 2 * LRU_W + ATT_QK, 2 * LRU_W + 2 * ATT_QK], axis=-1)
    rec = rg_lru(causal_depthwise_conv(xr, conv_w, conv_b), lru_wa, lru_ba, lru_wx, lru_bx, lru_lambda)
    rec = rec * jax.nn.gelu(yg, approximate=True)
    att = diff_attention(q, k, v, lq1, lk1, lq2, lk2, subln_g, rel_bias, lam_init)
    merged = jnp.concatenate([rec, att], axis=-1)
    return jnp.einsum('bse,ed->bsd', merged, w_out)


def swiglu(h, w_gate, w_up, w_down):
    g = jnp.einsum('bsd,df->bsf', h, w_gate)
    u = jnp.einsum('bsd,df->bsf', h, w_up)
    return jnp.einsum('bsf,fd->bsd', jax.nn.silu(g) * u, w_down)


def setup_inputs(seed: int = 0) -> dict:
    key = jax.random.key(seed)
    ks = jax.random.split(key, 32)
    L, D = DEPTH, D_MODEL

    def nrm(k, shape, scale):
        return jax.random.normal(k, shape, jnp.float32) * scale

    x = nrm(ks[0], (BATCH, SEQ, D), 1.0)
    p = nrm(ks[1], (L, BATCH, SEQ, PLE_DIM), 1.0)
    col_scale = jnp.concatenate([jnp.ones((IN_W - ATT_V,), jnp.float32),
                                 jnp.full((ATT_V,), BETA, jnp.float32)])
    w_in = nrm(ks[2], (L, D, IN_W), D ** -0.5) * col_scale
    conv_w = nrm(ks[3], (L, CONV_W, LRU_W), CONV_W ** -0.5)
    conv_b = nrm(ks[4], (L, LRU_W), 0.01)
    lru_wa = nrm(ks[5], (L, LRU_BLOCKS, LRU_BW, LRU_BW), LRU_BW ** -0.5)
    lru_ba = nrm(ks[6], (L, LRU_BLOCKS, LRU_BW), 0.01)
    lru_wx = nrm(ks[7], (L, LRU_BLOCKS, LRU_BW, LRU_BW), LRU_BW ** -0.5)
    lru_bx = nrm(ks[8], (L, LRU_BLOCKS, LRU_BW), 0.01)
    u = jax.random.uniform(ks[9], (L, LRU_W), jnp.float32, 0.9, 0.999)
    s = u ** (1.0 / LRU_C)
    lru_lambda = jnp.log(s) - jnp.log1p(-s)
    diff_lq1 = nrm(ks[10], (L, ATT_DQK), 0.1)
    diff_lk1 = nrm(ks[11], (L, ATT_DQK), 0.1)
    diff_lq2 = nrm(ks[12], (L, ATT_DQK), 0.1)
    diff_lk2 = nrm(ks[13], (L, ATT_DQK), 0.1)
    diff_subln_g = 1.0 + nrm(ks[14], (L, ATT_DV), 0.02)
    rel_bias = nrm(ks[15], (REL_BUCKETS, ATT_HEADS), 0.5)
    w_out = nrm(ks[16], (L, MIX_W, D), MIX_W ** -0.5 * BETA)
    ln1_g = 1.0 + nrm(ks[17], (L, D), 0.02)
    ln1_b = nrm(ks[18], (L, D), 0.01)
    w_ffn_gate = nrm(ks[19], (L, D, FFN_HIDDEN), D ** -0.5 * BETA)
    w_ffn_up = nrm(ks[20], (L, D, FFN_HIDDEN), D ** -0.5 * BETA)
    w_ffn_down = nrm(ks[21], (L, FFN_HIDDEN, D), FFN_HIDDEN ** -0.5 * BETA)
    ln2_g = 1.0 + nrm(ks[22], (L, D), 0.02)
    ln2_b = nrm(ks[23], (L, D), 0.01)
    w_ple_gate = nrm(ks[24], (L, D, D), D ** -0.5)
    b_ple_gate = nrm(ks[25], (L, D), 0.01)
    w_ple_proj = nrm(ks[26], (L, PLE_DIM, D), PLE_DIM ** -0.5 * BETA)
    ln3_g = 1.0 + nrm(ks[27], (L, D), 0.02)
    ln3_b = nrm(ks[28], (L, D), 0.01)
    return {"x": x, "p": p, "w_in": w_in, "conv_w": conv_w, "conv_b": conv_b,
            "lru_wa": lru_wa, "lru_ba": lru_ba, "lru_wx": lru_wx, "lru_bx": lru_bx,
            "lru_lambda": lru_lambda, "diff_lq1": diff_lq1, "diff_lk1": diff_lk1,
            "diff_lq2": diff_lq2, "diff_lk2": diff_lk2, "diff_subln_g": diff_subln_g,
            "rel_bias": rel_bias, "w_out": w_out, "ln1_g": ln1_g, "ln1_b": ln1_b,
            "w_ffn_gate": w_ffn_gate, "w_ffn_up": w_ffn_up, "w_ffn_down": w_ffn_down,
            "ln2_g": ln2_g, "ln2_b": ln2_b, "w_ple_gate": w_ple_gate, "b_ple_gate": b_ple_gate,
            "w_ple_proj": w_ple_proj, "ln3_g": ln3_g, "ln3_b": ln3_b}


def reference(x, p, w_in, conv_w, conv_b, lru_wa, lru_ba, lru_wx, lru_bx, lru_lambda,
              diff_lq1, diff_lk1, diff_lq2, diff_lk2, diff_subln_g, rel_bias, w_out,
              ln1_g, ln1_b, w_ffn_gate, w_ffn_up, w_ffn_down, ln2_g, ln2_b,
              w_ple_gate, b_ple_gate, w_ple_proj, ln3_g, ln3_b):
    h = x
    for i in range(DEPTH):
        lam_init = 0.8 - 0.6 * math.exp(-0.3 * i)
        m = hybrid_mixer(h, w_in[i], conv_w[i], conv_b[i], lru_wa[i], lru_ba[i], lru_wx[i], lru_bx[i],
                         lru_lambda[i], diff_lq1[i], diff_lk1[i], diff_lq2[i], diff_lk2[i],
                         diff_subln_g[i], rel_bias, w_out[i], lam_init)
        h = layer_norm(ALPHA * h + m, ln1_g[i], ln1_b[i])
        f = swiglu(h, w_ffn_gate[i], w_ffn_up[i], w_ffn_down[i])
        h = layer_norm(ALPHA * h + f, ln2_g[i], ln2_b[i])
        gate = jax.nn.sigmoid(jnp.einsum('bsd,de->bse', h, w_ple_gate[i]) + b_ple_gate[i])
        e = jnp.einsum('bsk,kd->bsd', p[i], w_ple_proj[i])
        h = layer_norm(ALPHA * h + gate * e, ln3_g[i], ln3_b[i])
    return h
```

```python
import contextlib
import math
import numpy as np
import concourse.bass as bass
import concourse.mybir as mybir
from concourse.bass_utils import run_bass_kernel_spmd

F32 = mybir.dt.float32
BF16 = mybir.dt.bfloat16
AF = mybir.ActivationFunctionType
ALU = mybir.AluOpType
AX = mybir.AxisListType

N_CORES = 8
D = 2048
SEQ = 2048
BATCH = 16
SPC = BATCH // N_CORES
DC = D // 128
IN_W = 5120
FFN = 5632
FC = FFN // 128
PLE = 256
NTB = SEQ // 128
ALPHA = 2.0 ** 0.25
LAM_INIT = 0.2
EPS = 1e-5
LRU_C = 8.0
NEG = -30000.0

ENGS = ("pe", "act", "dve", "pool", "sp")


class T:
    __slots__ = ("name", "w", "r", "rd")

    def __init__(self, name=""):
        self.name = name
        self.w = None
        self.r = {}
        self.rd = []


class Op:
    __slots__ = ("eng", "fn", "deps", "idx", "signal", "sigval", "dma", "sem", "semval", "ndma")

    def __init__(self, eng, fn, dma=False, ndma=1):
        self.eng = eng
        self.fn = fn
        self.deps = []
        self.signal = False
        self.sigval = 0
        self.dma = dma
        self.sem = None
        self.semval = 0
        self.ndma = ndma
        self.idx = 0


class Sched:
    def __init__(self, nc):
        self.nc = nc
        self.ops = {e: [] for e in ENGS}
        self.n_slots = 0
        self.all_ops = []
        self.pending = {}
        self.last_dma = {}

    def op(self, eng, fn, reads=(), writes=(), dma=False, dma_slot=None, ndma=1):
        o = Op(eng, fn, dma=dma, ndma=ndma)
        o.idx = len(self.ops[eng])
        best = {}
        dd = []

        def add(d):
            if d is None or d is o:
                return
            if d.dma:
                if d not in dd:
                    dd.append(d)
            else:
                b = best.get(d.eng)
                if b is None or d.idx > b.idx:
                    best[d.eng] = d

        for t in reads:
            add(t.w)
        for t in writes:
            add(t.w)
            for x in t.r.values():
                add(x)
            for x in t.rd:
                add(x)
        for x in self.pending.pop(eng, ()):
            add(x)
        o.deps = list(best.values()) + dd
        for t in reads:
            if dma:
                t.rd.append(o)
            else:
                t.r[eng] = o
        for t in writes:
            t.w = o
            t.r = {}
            t.rd = []
        if dma:
            o.sem = dma_slot
            self.last_dma[dma_slot] = o
        self.ops[eng].append(o)
        self.all_ops.append(o)
        return o

    def new_dma_slot(self):
        self.n_slots += 1
        return self.n_slots - 1

    def barrier(self):
        lasts = []
        for e in ENGS:
            for o in reversed(self.ops[e]):
                if not o.dma:
                    lasts.append(o)
                    break
        lasts.extend(self.last_dma.values())
        for e in ENGS:
            self.pending[e] = list(lasts) + list(self.pending.get(e, ()))

    def emit(self):
        nc = self.nc
        for o in self.all_ops:
            for d in o.deps:
                if (not d.dma) and d.eng == o.eng and o.eng == "pe":
                    continue
                d.signal = True
        for e in ENGS:
            c = 0
            for o in self.ops[e]:
                if o.dma:
                    continue
                if o.signal:
                    c += 1
                    o.sigval = c
        cnt = [0] * self.n_slots
        for o in self.all_ops:
            if o.dma:
                cnt[o.sem] += 16 * o.ndma
                o.semval = cnt[o.sem]
        with contextlib.ExitStack() as st:
            eng_sem = {e: st.enter_context(nc.semaphore("prog_" + e)) for e in ENGS}
            dsem = [st.enter_context(nc.semaphore("dma_%d" % i)) for i in range(self.n_slots)]
            block = st.enter_context(nc.Block())

            def run_engine(ename, eng):
                waited = {}
                for o in self.ops[ename]:
                    for d in o.deps:
                        if d.dma:
                            key = ("d", d.sem)
                            s, v = dsem[d.sem], d.semval
                        else:
                            if d.eng == ename and ename == "pe":
                                continue
                            key = ("e", d.eng)
                            s, v = eng_sem[d.eng], d.sigval
                        if waited.get(key, 0) >= v:
                            continue
                        waited[key] = v
                        eng.wait_ge(s, v)
                    ins = o.fn(eng)
                    if o.dma:
                        if not isinstance(ins, (list, tuple)):
                            ins = [ins]
                        assert len(ins) == o.ndma
                        for i_ in ins:
                            i_.then_inc(dsem[o.sem], 16)
                    elif o.signal:
                        ins.then_inc(eng_sem[ename], 1)
                for o in self.ops[ename]:
                    if o.dma:
                        key = ("d", o.sem)
                        if waited.get(key, 0) < o.semval:
                            waited[key] = o.semval
                            eng.wait_ge(dsem[o.sem], o.semval)

            @block.tensor
            def _(eng):
                run_engine("pe", eng)

            @block.scalar
            def _(eng):
                run_engine("act", eng)

            @block.vector
            def _(eng):
                run_engine("dve", eng)

            @block.gpsimd
            def _(eng):
                run_engine("pool", eng)

            @block.sync
            def _(eng):
                run_engine("sp", eng)


def _bucket(n):
    max_exact = 16
    nf = np.maximum(n, 1).astype(np.float32)
    large = max_exact + (np.log(nf / np.float32(max_exact)) / np.float32(math.log(128 / max_exact))
                         * np.float32(32 - max_exact)).astype(np.int32)
    large = np.minimum(large, 31)
    return np.where(n < max_exact, n, large)


def _consts():
    cm = np.zeros((128, 6 * 128), np.float32)
    cm[:, 0:128] = np.eye(128)
    cm[:, 128:256] = np.eye(128)[::-1]
    cm[0:64, 256:384] = 1.0
    cm[64:128, 384:512] = 1.0
    cm[:, 512:640] = 1.0 / D
    cm[:, 640:768] = 1.0
    oh = np.zeros((32, 384), np.float32)
    mrow = np.zeros((8, 384), np.float32)
    for m in range(384):
        n = m - 127
        if n < 0:
            mrow[:, m] = NEG
        else:
            b = int(_bucket(np.array([n]))[0])
            oh[b, m] = 1.0
    return cm, oh, mrow


class _Stop(Exception):
    pass


def build(nseq=SPC, debug=False, stop=None):
    nc = bass.Bass("TRN2", target_bir_lowering=False)
    NTOK = nseq * SEQ
    dram = lambda name, shape, kind="ExternalInput": nc.dram_tensor(name, shape, F32, kind=kind).ap()
    X = dram("x", [NTOK, D])
    P_ = dram("p", [NTOK, PLE])
    W_IN = dram("w_in", [D, IN_W])
    W_OUT = dram("w_out", [D, D])
    W_G = dram("w_gate", [D, FFN])
    W_U = dram("w_up", [D, FFN])
    W_D = dram("w_down", [FFN, D])
    W_PG = dram("w_pgate", [D, D])
    W_PP = dram("w_pproj", [PLE, D])
    LRUW = dram("lruw", [128, 2 * 8 * 128])
    PVEC = dram("pvec", [128, 176])
    BROW = dram("brow", [1, 640])
    RELB = dram("relb", [32, 8])
    CMAT = dram("cmat", [128, 768])
    ONEH = dram("oneh", [32, 384])
    MROW = dram("mrow", [8, 384])
    OUT = dram("out", [NTOK, D], kind="ExternalOutput")
    SCR = dram("scr", [8, 384], kind="Internal")
    DBG = dram("dbg", [128, 16 * SEQ], kind="ExternalOutput") if debug else None

    S = Sched(nc)
    with contextlib.ExitStack() as st:
        sbt = lambda name, shape, dt=F32: st.enter_context(nc.sbuf_tensor(name, shape, dt))
        identF = sbt("identF", [128, 128], F32)
        cb = sbt("cb", [128, 768], BF16)
        identB, Jb, sel0, sel1, meanm, onesb = (cb[:, i * 128:(i + 1) * 128] for i in range(6))
        pv = sbt("pv", [128, 176], F32)
        der = sbt("der", [128, 96], F32)
        lruw = sbt("lruw_sb", [128, 2 * 8 * 128], BF16)
        HB = sbt("HB", [128, 8 * 256], BF16)
        gsub = sbt("gsub", [128, 128], F32)
        sm = sbt("sm", [128, 64], F32)
        mergedT = sbt("mergedT", [128, DC * SEQ], BF16)
        NSLOT = 4
        wsl = [sbt("wslot%d" % i, [128, 16 * 128], BF16) for i in range(NSLOT)]
        import os
        ARENA = int(os.environ.get('ARENA', '59400'))
        arena = sbt("arena", [128, ARENA], BF16)

        t_const = T("const")
        t_wsl = [T("wsl%d" % i) for i in range(NSLOT)]
        d_wsl = [S.new_dma_slot() for _ in range(NSLOT)]
        mT = [[T("mT%d_%d" % (e, tt)) for tt in range(4)] for e in range(DC)]

        PB = [st.enter_context(nc.psum_tensor("pb%d" % i, [128, 512], F32)) for i in range(7)]
        t_PB = [T("pb%d" % i) for i in range(7)]
        pbf_all = st.enter_context(nc.psum_tensor("pbf0", [128, 1024], BF16))
        PBF = [pbf_all[:, 0:512]]
        t_PBF = [T("pbf0")]

        class Carver:
            def __init__(self):
                self.off = 0

            def take(self, nelem, dt):
                if dt == F32:
                    if self.off % 2:
                        self.off += 1
                    a = arena[:, self.off:self.off + 2 * nelem].bitcast(F32)
                    self.off += 2 * nelem
                else:
                    a = arena[:, self.off:self.off + nelem]
                    self.off += nelem
                assert self.off <= ARENA, self.off
                return a

        def dma(q, out, in_, reads=(), writes=(), slot=None, allow=False):
            if slot is None:
                slot = S.new_dma_slot()
            if allow:
                f = lambda e: e.dma_start(out=out, in_=in_, allow_slow_non_contiguous=True)
            else:
                f = lambda e: e.dma_start(out=out, in_=in_)
            return S.op(q, f, reads=reads, writes=writes, dma=True, dma_slot=slot)

        wi = [0]

        def load_w(src_ap, nk):
            s = wi[0] % NSLOT
            wi[0] += 1
            dst = wsl[s][:, 0:nk * 128].rearrange("p (k c) -> p k c", c=128)
            S.op("pool", lambda e: e.dma_start(out=dst, in_=src_ap), writes=[t_wsl[s]], dma=True, dma_slot=d_wsl[s])
            return s

        def wk(s, k):
            return wsl[s][:, k * 128:(k + 1) * 128]

        def mm(ps_ap, t_ps, lhsT, rhs, start, stop, reads):
            return S.op("pe", lambda e: e.matmul(ps_ap, lhsT=lhsT, rhs=rhs, start=start, stop=stop),
                        reads=reads, writes=[t_ps])

        def act(out, in_, func, reads, writes, bias=None, scale=None, accum_out=None):
            kw = {}
            if bias is not None:
                kw["bias"] = bias
            if scale is not None:
                kw["scale"] = scale
            if accum_out is not None:
                kw["accum_out"] = accum_out
            return S.op("act", lambda e: e.activation(out=out, in_=in_, func=func, **kw), reads=reads, writes=writes)

        def ts(eng, out, in0, s1, s2, op0, op1, reads, writes):
            if s2 is None:
                return S.op(eng, lambda e: e.tensor_scalar(out=out, in0=in0, scalar1=s1, scalar2=None, op0=op0),
                            reads=reads, writes=writes)
            return S.op(eng, lambda e: e.tensor_scalar(out=out, in0=in0, scalar1=s1, scalar2=s2, op0=op0, op1=op1),
                        reads=reads, writes=writes)

        def tt(eng, out, in0, in1, op, reads, writes):
            return S.op(eng, lambda e: e.tensor_tensor(out=out, in0=in0, in1=in1, op=op), reads=reads, writes=writes)

        def stt(out, in0, scalar, in1, op0, op1, reads, writes):
            return S.op("dve", lambda e: e.scalar_tensor_tensor(out=out, in0=in0, scalar=scalar, in1=in1, op0=op0, op1=op1),
                        reads=reads, writes=writes)

        def cp(eng, out, in_, reads, writes):
            if eng == "act":
                return act(out, in_, AF.Identity, reads, writes)
            return S.op(eng, lambda e: e.tensor_copy(out=out, in_=in_), reads=reads, writes=writes)

        evi = [0]

        def evac(out, in_, reads, writes):
            evi[0] += 1
            import os
            if os.environ.get("EVAC_DVE"):
                return cp("dve", out, in_, reads, writes)
            return cp("act" if evi[0] % 2 else "dve", out, in_, reads, writes)

        try:
            car0 = Carver()
            bc = car0.take(640, F32)
            dma("sp", identF[:], CMAT[:, 0:128], writes=[t_const])
            dma("pool", cb[:], CMAT[:, 0:768], writes=[t_const])
            dma("sp", pv[:], PVEC[:, :], writes=[t_const])
            dma("sp", bc, BROW.partition_broadcast(128), writes=[t_const])
            dma("pool", lruw[:], LRUW[:, :], writes=[t_const])
            if stop == 'S1':
                raise _Stop()
            CW = lambda n, j: pv[:, n * 4 + j:n * 4 + j + 1]
            CBc = lambda n: pv[:, 32 + n:33 + n]
            BA = lambda n: pv[:, 40 + n:41 + n]
            BX = lambda n: pv[:, 48 + n:49 + n]
            LNP = lambda k, c: pv[:, 64 + k * 16 + c:64 + k * 16 + c + 1]
            LSC = lambda n: der[:, n:n + 1]
            LSC2 = lambda n: der[:, 8 + n:9 + n]
            GA_ = lambda k, c: der[:, 16 + k * 16 + c:16 + k * 16 + c + 1]
            NEGLAM = der[:, 80:81]
            EPSC = der[:, 81:82]
            ONEC = der[:, 82:83]
            B8S = lambda h: der[:, 84 + h:85 + h]
            GCOL = der[:, 83:84]
            tc_ = t_const
            S.op("dve", lambda e: e.memset(der[:, 81:82], EPS), writes=[tc_])
            S.op("dve", lambda e: e.memset(der[:, 82:83], 1.0), writes=[tc_])
            ts("dve", der[:, 16:80], pv[:, 64:128], ALPHA, None, ALU.mult, None, [tc_], [tc_])
            L = pv[:, 56:64]
            m_ = sm[:, 0:8]; z_ = sm[:, 8:16]; w_ = sm[:, 16:24]; w2 = sm[:, 24:32]; pl = sm[:, 32:40]
            ts("dve", m_, L, -1.0, None, ALU.mult, None, [tc_], [tc_])
            tt("dve", z_, m_, L, ALU.max, [tc_], [tc_])
            ts("dve", m_, m_, 0.0, None, ALU.max, None, [tc_], [tc_])
            act(z_, z_, AF.Exp, [tc_], [tc_], scale=-1.0)
            ts("dve", w_, z_, 2.0, None, ALU.add, None, [tc_], [tc_])
            S.op("dve", lambda e: e.reciprocal(out=w_, in_=w_), reads=[tc_], writes=[tc_])
            tt("dve", w_, w_, z_, ALU.mult, [tc_], [tc_])
            tt("dve", w2, w_, w_, ALU.mult, [tc_], [tc_])
            ts("dve", pl, w2, 1.0 / 9.0, 1.0 / 7.0, ALU.mult, ALU.add, [tc_], [tc_])
            for cf in (1.0 / 5.0, 1.0 / 3.0, 1.0):
                tt("dve", pl, pl, w2, ALU.mult, [tc_], [tc_])
                ts("dve", pl, pl, cf, None, ALU.add, None, [tc_], [tc_])
            tt("dve", pl, pl, w_, ALU.mult, [tc_], [tc_])
            ts("dve", pl, pl, 2.0, None, ALU.mult, None, [tc_], [tc_])
            tt("dve", pl, pl, m_, ALU.add, [tc_], [tc_])
            ts("dve", der[:, 0:8], pl, -LRU_C, None, ALU.mult, None, [tc_], [tc_])
            ts("dve", der[:, 8:16], pl, -2.0 * LRU_C, None, ALU.mult, None, [tc_], [tc_])
            if stop == 'S2':
                raise _Stop()
            pr = sm[:, 40:42]
            tmpl = sm[:, 42:44]
            for k in range(2):
                S.op("dve", lambda e, k=k: e.tensor_tensor(out=gsub[:, 0:64], in0=bc[:, k * 128:k * 128 + 64],
                                                           in1=bc[:, k * 128 + 64:k * 128 + 128], op=ALU.mult),
                     reads=[tc_], writes=[tc_])
                S.op("dve", lambda e, k=k: e.reduce_sum(out=pr[:, k:k + 1], in_=gsub[:, 0:64], axis=AX.X), reads=[tc_], writes=[tc_])
            act(pr, pr, AF.Exp, [tc_], [tc_])
            tt("dve", tmpl[:, 0:1], pr[:, 1:2], pr[:, 0:1], ALU.subtract, [tc_], [tc_])
            ts("dve", der[:, 80:81], tmpl[:, 0:1], -LAM_INIT, None, ALU.add, None, [tc_], [tc_])
            dma("sp", der[:, 83:84], BROW[0:1, 256:384].rearrange("a d -> d a"), writes=[tc_], allow=True)
            ts("dve", der[:, 83:84], der[:, 83:84], 1.0 - LAM_INIT, None, ALU.mult, None, [tc_], [tc_])
            ts("dve", gsub[:], bc[:, 256:384], 1.0 - LAM_INIT, None, ALU.mult, None, [tc_], [tc_])
            bm = sm[:, 44:52]
            S.op("dve", lambda e: e.reduce_max(out=bm, in_=bc[:, 384:640].rearrange("p (b h) -> p h b", h=8), axis=AX.X),
                 reads=[tc_], writes=[tc_])
            tt("dve", bm, bm, bc[:, 384 + 248:384 + 256], ALU.subtract, [tc_], [tc_])
            ts("dve", der[:, 84:92], bm, 0.0, None, ALU.max, None, [tc_], [tc_])
            if stop == 'S3':
                raise _Stop()
            relb_sb = car0.take(8, F32)[0:32, :]
            oneh_sb = car0.take(384, F32)[0:32, :]
            mrow_sb = car0.take(384, F32)[0:8, :]
            c8 = car0.take(1, F32)[0:8, :]
            vec8 = car0.take(384, F32)[0:8, :]
            hst = car0.take(256, F32)
            t_set = T("setup")
            dma("sp", relb_sb, RELB[:, :], writes=[t_set])
            dma("sp", oneh_sb, ONEH[:, :], writes=[t_set])
            dma("sp", mrow_sb, MROW[:, :], writes=[t_set])
            dma("sp", c8, RELB[31:32, :].rearrange("a h -> h a"), writes=[t_set], allow=True)
            mm(PB[0][0:8, 0:384], t_PB[0], relb_sb, oneh_sb, True, True, [t_set])
            ts("dve", vec8, PB[0][0:8, 0:384], c8, 8.0, ALU.subtract, ALU.mult, [t_PB[0], t_set], [t_set])
            tt("dve", vec8, vec8, mrow_sb, ALU.add, [t_set], [t_set])
            if stop == 'S4':
                raise _Stop()
            t_scr = T("scr")
            dma("sp", SCR[:, :], vec8, reads=[t_set], writes=[t_scr])
            for h in range(8):
                hk = bass.AP(tensor=SCR.tensor, offset=h * 384, ap=[[1, 128], [1, 256]])
                dma("sp", hst, hk, reads=[t_scr], writes=[t_set])
                cp("dve", HB[:, h * 256:(h + 1) * 256], hst, [t_set], [tc_])
            S.barrier()
            if stop == 'setup':
                raise _Stop()

            Wv = W_IN.rearrange("(c p) e -> p c e", p=128)
            Wo = W_OUT.rearrange("(c p) e -> p c e", p=128)
            Wg = W_G.rearrange("(c p) e -> p c e", p=128)
            Wu = W_U.rearrange("(c p) e -> p c e", p=128)
            Wd = W_D.rearrange("(c p) e -> p c e", p=128)
            Wpg = W_PG.rearrange("(c p) e -> p c e", p=128)
            Wpp = W_PP.rearrange("(c p) e -> p c e", p=128)
            mview = lambda e, c0, n: mergedT[:, e * SEQ + c0:e * SEQ + c0 + n]
            d_xin = [S.new_dma_slot() for _ in range(2)]
            d_xbf = [S.new_dma_slot() for _ in range(4)]
            d_out = [S.new_dma_slot() for _ in range(2)]
            d_p = S.new_dma_slot()

            for s in range(nseq):
                tok0 = s * SEQ
                car = Carver()
                xT = car.take(DC * SEQ, BF16)
                xTv = lambda d, c0, n: xT[:, d * SEQ + c0:d * SEQ + c0 + n]
                t_xT = [T("xT%d" % i) for i in range(4)]
                mark = car.off
                NXB = 4
                xbf = [car.take(D, BF16) for _ in range(NXB)]
                t_xbf = [T("xbf%d" % i) for i in range(NXB)]
                abanks = [(PB[k_][:, 0:256].bitcast(BF16), t_PB[k_]) for k_ in range(7)] + [(PBF[0], t_PBF[0])]
                gi_ = 0
                for blk in range(NTB):
                    sl = blk % NXB
                    S.op("pool", lambda e, sl=sl, blk=blk, tok0=tok0, xbf=xbf: [e.dma_start(out=xbf[sl][:, hh * 1024:(hh + 1) * 1024],
                                                                        in_=X[tok0 + blk * 128:tok0 + (blk + 1) * 128, hh * 1024:(hh + 1) * 1024])
                                                            for hh in range(2)],
                         writes=[t_xbf[sl]], dma=True, dma_slot=d_xbf[sl], ndma=2)
                    for g in range(4):
                        bank_ap, t_bank = abanks[gi_ % len(abanks)]
                        gi_ += 1
                        for a in range(4):
                            c = 4 * g + a
                            S.op("pe", lambda e, bank_ap=bank_ap, a=a, c=c, sl=sl, xbf=xbf: e.transpose(bank_ap[:, a * 128:(a + 1) * 128],
                                                                                                     xbf[sl][:, c * 128:(c + 1) * 128], identB),
                                 reads=[t_xbf[sl], t_const], writes=[t_bank])
                        dst = xT[:, 4 * g * SEQ:(4 * g + 4) * SEQ].rearrange("p (a t) -> p a t", t=SEQ)[:, :, blk * 128:(blk + 1) * 128]
                        src = bank_ap.rearrange("p (a t) -> p a t", t=128)
                        evac(dst, src, [t_bank], [t_xT[blk // 4]])
                S.barrier()
                car.off = mark
                if stop == 'A0':
                    raise _Stop()

                def inproj_mm(col0):
                    sl = load_w(Wv[:, :, col0:col0 + 128], DC)
                    for d in range(DC):
                        for t4 in range(4):
                            mm(PB[t4][:, :], t_PB[t4], wk(sl, d), xTv(d, t4 * 512, 512), d == 0, d == DC - 1,
                               [t_wsl[sl], t_xT[t4]])

                def inproj_evac(dst_fn, dst_tiles):
                    for t4 in range(4):
                        evac(dst_fn(t4), PB[t4][:, :], [t_PB[t4]], dst_tiles)

                xrp2 = [car.take(SEQ + 4, F32) for _ in range(2)]; t_xrp2 = [T("xrp0"), T("xrp1")]
                ygb2 = [car.take(SEQ, F32) for _ in range(2)]; t_ygb2 = [T("ygb0"), T("ygb1")]
                QW = 512
                LS = []
                for si in range(2):
                    LS.append(dict(conv=car.take(QW, F32), ga=car.take(QW, F32), gx=car.take(QW, F32), A=car.take(QW, F32),
                                   convb=car.take(QW, BF16), t_conv=T("conv%d" % si), t_ga=T("ga%d" % si), t_gx=T("gx%d" % si),
                                   t_A=T("A%d" % si), t_convb=T("convb%d" % si)))
                carry4 = sm[:, 24:28]; t_carry = [T("carry%d" % i) for i in range(4)]
                for b_ in range(2):
                    S.op("dve", lambda e, b_=b_: e.memset(xrp2[b_][:, 0:4], 0.0), writes=[t_xrp2[b_]])
                gbank = [0]

                def lru_math(n, q):
                    L_ = LS[q % 2]
                    conv_h, ga_h, gx_h, A_h, convb = L_["conv"], L_["ga"], L_["gx"], L_["A"], L_["convb"]
                    t_conv, t_ga, t_gx, t_A, t_convb = L_["t_conv"], L_["t_ga"], L_["t_gx"], L_["t_A"], L_["t_convb"]
                    xrp = xrp2[n % 2]; t_xrp = t_xrp2[n % 2]
                    ygb = ygb2[n % 2]; t_ygb = t_ygb2[n % 2]
                    c0 = q * QW
                    ts("dve", conv_h, xrp[:, c0 + 3:c0 + 3 + QW], CW(n, 3), CBc(n), ALU.mult, ALU.add, [t_xrp, tc_], [t_conv]); yield
                    for j in (2, 1, 0):
                        stt(conv_h, xrp[:, c0 + j:c0 + j + QW], CW(n, j), conv_h, ALU.mult, ALU.add, [t_xrp, t_conv, tc_], [t_conv]); yield
                    cp("act", convb, conv_h, [t_conv], [t_convb]); yield
                    for g, (dstb, t_d, bcol) in enumerate(((ga_h, t_ga, BA(n)), (gx_h, t_gx, BX(n)))):
                        pbk = 4 + (gbank[0] % 3); gbank[0] += 1
                        wg_ = lruw[:, (g * 8 + n) * 128:(g * 8 + n + 1) * 128]
                        mm(PB[pbk][:, :], t_PB[pbk], wg_, convb, True, True, [tc_, t_convb])
                        act(dstb, PB[pbk][:, :], AF.Sigmoid, [t_PB[pbk], tc_], [t_d], bias=bcol); yield
                    act(A_h, ga_h, AF.Exp, [t_ga, tc_], [t_A], scale=LSC(n)); yield
                    act(ga_h, ga_h, AF.Exp, [t_ga, tc_], [t_ga], scale=LSC2(n)); yield
                    ts("dve", ga_h, ga_h, -1.0, 1.0, ALU.mult, ALU.add, [t_ga], [t_ga]); yield
                    ts("dve", ga_h, ga_h, 1e-12, None, ALU.max, None, [t_ga], [t_ga]); yield
                    act(ga_h, ga_h, AF.Ln, [t_ga], [t_ga]); yield
                    act(ga_h, ga_h, AF.Exp, [t_ga], [t_ga], scale=0.5); yield
                    tt("dve", gx_h, gx_h, ga_h, ALU.mult, [t_gx, t_ga], [t_gx]); yield
                    tt("dve", gx_h, gx_h, conv_h, ALU.mult, [t_gx, t_conv], [t_gx]); yield
                    if q == 0:
                        init = 0.0; rd = [t_A, t_gx]
                    elif q % 2 == 1:
                        other = LS[(q - 1) % 2]
                        init = other["conv"][:, QW - 1:QW]; rd = [t_A, t_gx, other["t_conv"]]
                    else:
                        init = carry4[:, q - 1:q]; rd = [t_A, t_gx, t_carry[q - 1]]
                    S.op("dve", lambda e, init=init, conv_h=conv_h, A_h=A_h, gx_h=gx_h: e.tensor_tensor_scan(out=conv_h, data0=A_h, data1=gx_h, initial=init,
                                                                           op0=ALU.mult, op1=ALU.add),
                         reads=rd, writes=[t_conv]); yield
                    if q == 1:
                        cp("dve", carry4[:, q:q + 1], conv_h[:, QW - 1:QW], [t_conv], [t_carry[q]])
                    yield
                    yg = ygb[:, c0:c0 + QW]
                    act(ga_h, yg, AF.Square, [t_ygb], [t_ga]); yield
                    ts("dve", ga_h, ga_h, 0.044715, 1.0, ALU.mult, ALU.add, [t_ga], [t_ga]); yield
                    tt("dve", ga_h, ga_h, yg, ALU.mult, [t_ga, t_ygb], [t_ga]); yield
                    act(ga_h, ga_h, AF.Sigmoid, [t_ga], [t_ga], scale=1.5957691216057308); yield
                    tt("dve", ga_h, ga_h, yg, ALU.mult, [t_ga, t_ygb], [t_ga]); yield
                    tt("dve", mview(n, c0, QW), ga_h, conv_h, ALU.mult, [t_ga, t_conv], [mT[n][q]]); yield

                def inproj_units(col0, dst_fn, dst_tile):
                    hold = {}
                    res = []
                    for t4 in range(4):
                        def mmu(t4=t4):
                            if t4 == 0:
                                hold["sl"] = load_w(Wv[:, :, col0:col0 + 128], DC)
                            sl = hold["sl"]
                            for d in range(DC):
                                mm(PB[t4][:, :], t_PB[t4], wk(sl, d), xTv(d, t4 * 512, 512), d == 0, d == DC - 1, [t_wsl[sl], t_xT[t4]])

                        def evu(t4=t4):
                            evac(dst_fn(t4), PB[t4][:, :], [t_PB[t4]], [dst_tile])
                        res.append((mmu, evu))
                    return res

                def lru_pair(n, qa, qb, units):
                    ga_, gb_ = lru_math(n, qa), lru_math(n, qb)
                    alive = [ga_, gb_]
                    step = 0
                    sched_ = {}
                    for i_, (mmu, evu) in enumerate(units):
                        sched_.setdefault(1 + 4 * i_, []).append(mmu)
                        sched_.setdefault(1 + 4 * i_ + 7, []).append(evu)
                    while alive or any(k_ >= step for k_ in sched_):
                        for g_ in list(alive):
                            try:
                                next(g_)
                            except StopIteration:
                                alive.remove(g_)
                        for f_ in sched_.pop(step, []):
                            f_()
                        step += 1
                        if not alive and not sched_:
                            break

                xr_dst = lambda n: (lambda t4: xrp2[n % 2][:, 3 + t4 * 512:3 + (t4 + 1) * 512])
                yg_dst = lambda n: (lambda t4: ygb2[n % 2][:, t4 * 512:(t4 + 1) * 512])
                inproj_mm(0); inproj_evac(xr_dst(0), [t_xrp2[0]])
                inproj_mm(1024); inproj_evac(yg_dst(0), [t_ygb2[0]])
                for n in range(8):
                    ux = inproj_units((n + 1) * 128, xr_dst(n + 1), t_xrp2[(n + 1) % 2]) if n + 1 < 8 else []
                    uy = inproj_units(1024 + (n + 1) * 128, yg_dst(n + 1), t_ygb2[(n + 1) % 2]) if n + 1 < 8 else []
                    lru_pair(n, 0, 1, ux)
                    lru_pair(n, 2, 3, uy)
                S.barrier()
                car.off = mark
                if stop == 'A1':
                    raise _Stop()

                qT2 = [car.take(SEQ, BF16) for _ in range(2)]; t_q2 = [T("qT0"), T("qT1")]
                kT2 = [car.take(SEQ, BF16) for _ in range(2)]; t_k2 = [T("kT0"), T("kT1")]
                vT2 = [car.take(SEQ, BF16) for _ in range(2)]; t_v2 = [T("vT0"), T("vT1")]
                vaug2 = [car.take(NTB * 130, BF16) for _ in range(2)]; t_va2 = [T("va0"), T("va1")]
                vav2 = [v_.rearrange("p (j c) -> p j c", c=130) for v_ in vaug2]
                NET = 4
                ET = [car.take(512, BF16) for _ in range(NET)]; t_ET = [T("ET%d" % i) for i in range(NET)]
                tAB = [car.take(512, F32), car.take(512, F32)]
                rcp = car.take(512, F32)
                sqb = car.take(512, BF16)
                t_ep = T("ep")
                segidx = {}
                for Tq_ in range(4):
                    for c_ in range(2):
                        segidx[(Tq_, c_)] = len(segidx)
                mx = sm[:, 54:58]
                nrm = sm[:, 58:62]
                nsh2 = [sm[:, 20:22], sm[:, 22:24]]
                t_nsh2 = [T("nsh0"), T("nsh1")]
                t_sm = T("sm")
                for b_ in range(2):
                    S.op("dve", lambda e, b_=b_: e.memset(vav2[b_][:, :, 128:130], 1.0), writes=[t_va2[b_]])
                LA = 1
                BGB = 6
                deferred = []

                def prologue_units(h):
                    hp = h % 2
                    qT, kT, vT = qT2[hp], kT2[hp], vT2[hp]
                    t_q, t_k, t_v, t_va = t_q2[hp], t_k2[hp], t_v2[hp], t_va2[hp]
                    vav = vav2[hp]
                    nsh = nsh2[hp]; t_nsh = t_nsh2[hp]
                    units = []
                    for (col0, dst, t_dst) in ((2048 + h * 128, qT, t_q), (3072 + h * 128, kT, t_k), (4096 + h * 128, vT, t_v)):
                        hold = {}

                        def ld(col0=col0, hold=hold):
                            hold["sl"] = load_w(Wv[:, :, col0:col0 + 128], DC)
                        for t4 in range(4):
                            def u(t4=t4, dst=dst, t_dst=t_dst, hold=hold, ld=ld):
                                if t4 == 0:
                                    ld()
                                sl = hold["sl"]
                                for d in range(DC):
                                    mm(PB[BGB][:, :], t_PB[BGB], wk(sl, d), xTv(d, t4 * 512, 512), d == 0, d == DC - 1,
                                       [t_wsl[sl], t_xT[t4]])
                                cp("dve", dst[:, t4 * 512:(t4 + 1) * 512], PB[BGB][:, :], [t_PB[BGB]], [t_dst])
                            units.append(u)
                    for g in range(4):
                        def u(g=g):
                            for a in range(4):
                                blk = 4 * g + a
                                S.op("pe", lambda e, a=a, blk=blk: e.transpose(PBF[0][:, a * 128:(a + 1) * 128],
                                                                              vT[:, blk * 128:(blk + 1) * 128], identB),
                                     reads=[t_v, t_const], writes=[t_PBF[0]])
                            cp("dve", vav[:, 4 * g:4 * g + 4, 0:128], PBF[0][:, :].rearrange("p (a t) -> p a t", t=128), [t_PBF[0]], [t_va])
                        units.append(u)
                    sq = vT
                    for qi, (src, t_src) in enumerate(((qT, t_q), (kT, t_k))):
                        for c, sel in enumerate((sel0, sel1)):
                            def u(qi=qi, c=c, sel=sel, src=src, t_src=t_src):
                                if c == 0:
                                    act(sq, src, AF.Square, [t_src, t_v], [t_v])
                                for t4 in range(4):
                                    mm(PB[BGB][:, :], t_PB[BGB], sel, sq[:, t4 * 512:(t4 + 1) * 512], True, True, [t_v, t_const])
                                    S.op("dve", lambda e, t4=t4: e.reduce_max(out=mx[:, t4:t4 + 1], in_=PB[BGB][:, :], axis=AX.X),
                                         reads=[t_PB[BGB]], writes=[t_sm])
                                S.op("dve", lambda e, qi=qi, c=c: e.reduce_max(out=nrm[:, qi * 2 + c:qi * 2 + c + 1], in_=mx, axis=AX.X),
                                     reads=[t_sm], writes=[t_sm])
                            units.append(u)

                    def fin():
                        tt("dve", nsh, nrm[:, 0:2], nrm[:, 2:4], ALU.mult, [t_sm], [t_nsh])
                        ts("dve", nsh, nsh, 1e-30, None, ALU.max, None, [t_nsh], [t_nsh])
                        act(nsh, nsh, AF.Ln, [t_nsh], [t_nsh])
                        act(nsh, nsh, AF.Exp, [t_nsh], [t_nsh], scale=0.5)
                        ts("dve", nsh, nsh, -0.125 * 1.02, None, ALU.mult, None, [t_nsh], [t_nsh])
                        ts("dve", nsh, nsh, B8S(h), None, ALU.subtract, None, [t_nsh, tc_], [t_nsh])
                    units.append(fin)
                    return units

                for u_ in prologue_units(0):
                    u_()
                for h in range(8):
                    hp = h % 2
                    qT, kT = qT2[hp], kT2[hp]
                    t_q, t_k, t_va = t_q2[hp], t_k2[hp], t_va2[hp]
                    vav = vav2[hp]
                    nsh = nsh2[hp]; t_nsh = t_nsh2[hp]
                    bg = prologue_units(h + 1) if h + 1 < 8 else []
                    hb_ = HB[:, h * 256:(h + 1) * 256]
                    iters = [(Tq, c, j) for Tq in range(4) for c in range(2) for j in range(4 * Tq + 4)]
                    bg_every = max(1, len(iters) // (len(bg) + 1)) if bg else 0

                    def emit_qk(k):
                        Tq, c, j = iters[k]
                        rs = slice(c * 64, (c + 1) * 64)
                        qlo = max(j, 4 * Tq)
                        ncols = (4 * Tq + 4 - qlo) * 128
                        pss = 4 + (k % 2)
                        es = k % NET
                        near = j >= 4 * Tq - 1
                        mm(PB[pss][:, 0:ncols], t_PB[pss], kT[rs, j * 128:(j + 1) * 128], qT[rs, qlo * 128:(4 * Tq + 4) * 128],
                           True, not near, [t_q, t_k])
                        if j >= 4 * Tq:
                            nb = min(256, ncols)
                            mm(PB[pss][:, 0:nb], t_PB[pss], Jb, hb_[:, 0:nb], False, True, [t_const])
                        elif j == 4 * Tq - 1:
                            mm(PB[pss][:, 0:128], t_PB[pss], Jb, hb_[:, 128:256], False, True, [t_const])
                        act(ET[es][:, 0:ncols], PB[pss][:, 0:ncols], AF.Exp, [t_PB[pss], t_nsh], [t_ET[es]],
                            bias=nsh[:, c:c + 1], scale=0.125)

                    def emit_pv(k):
                        Tq, c, j = iters[k]
                        qlo = max(j, 4 * Tq)
                        ncols = (4 * Tq + 4 - qlo) * 128
                        col0 = (qlo - 4 * Tq) * 128
                        es = k % NET
                        sgi = segidx[(Tq, c)]
                        ob, db = 2 * (sgi % 2), 2 * (sgi % 2) + 1
                        last = (j == 4 * Tq + 3)
                        mm(PB[ob][:, col0:col0 + ncols], t_PB[ob], vav[:, j, 0:128], ET[es][:, 0:ncols], j == 0, last, [t_ET[es], t_va])
                        mm(PB[db][:, col0:col0 + ncols], t_PB[db], onesb, ET[es][:, 0:ncols], j == 0, last, [t_ET[es], t_const])
                        if last:
                            tC = tAB[c]
                            act(rcp, PB[db][:, :], AF.Ln, [t_PB[db], t_ep], [t_ep])
                            act(rcp, rcp, AF.Exp, [t_ep], [t_ep], scale=-1.0)
                            tt("dve", tC, PB[ob][:, :], rcp, ALU.mult, [t_PB[ob], t_ep], [t_ep])
                            if c == 1:
                                epilogue(Tq, db)

                    def epilogue(Tq, db):
                        for dd_ in list(deferred):
                            deferred.remove(dd_)
                            dd_[1]()
                        tA, tB = tAB
                        stt(tA, tB, NEGLAM, tA, ALU.mult, ALU.add, [t_ep, tc_], [t_ep])
                        tt("dve", sqb, tA, tA, ALU.mult, [t_ep], [t_ep])

                        def part2(h=h, Tq=Tq, db=db):
                            mm(PB[db][:, :], t_PB[db], onesb, sqb, True, True, [t_ep, t_const])
                            ts("dve", rcp, PB[db][:, :], 1.0 / 128.0, EPS, ALU.mult, ALU.add, [t_PB[db], t_ep], [t_ep])
                            act(rcp, rcp, AF.Ln, [t_ep], [t_ep])
                            act(rcp, rcp, AF.Exp, [t_ep], [t_ep], scale=-0.5)
                            stt(mview(8 + h, Tq * 512, 512), tA, GCOL, rcp, ALU.mult, ALU.mult, [t_ep, tc_], [mT[8 + h][Tq], t_ep])
                        deferred.append([3, part2])

                    for k in range(len(iters) + LA):
                        if k < len(iters):
                            emit_qk(k)
                        if k - LA >= 0:
                            emit_pv(k - LA)
                        for dd_ in list(deferred):
                            dd_[0] -= 1
                            if dd_[0] <= 0:
                                deferred.remove(dd_)
                                dd_[1]()
                        if bg and k % bg_every == bg_every - 1:
                            bg.pop(0)()
                    while bg:
                        bg.pop(0)()
                for dd_ in deferred:
                    dd_[1]()
                deferred = []
                S.barrier()

                if debug:
                    for e_ in range(DC):
                        dma("pool", DBG[:, e_ * SEQ:(e_ + 1) * SEQ], mview(e_, 0, SEQ), reads=[mT[e_][i] for i in range(4)])
                    S.barrier()
                    continue

                car = Carver()
                TT = 512
                y = car.take(DC * TT, F32)
                yv = lambda c: y[:, c * TT:(c + 1) * TT]
                t_y = [T("y%d" % c) for c in range(DC)]
                hb = car.take(DC * TT, BF16)
                hbv = lambda c: hb[:, c * TT:(c + 1) * TT]
                t_hb = [T("hb%d" % c) for c in range(DC)]
                NA = 15
                ab = car.take(16 * TT, BF16)
                abv = lambda j: ab[:, j * TT:(j + 1) * TT]
                t_ab = [T("a%d" % j) for j in range(NA)]
                t_ab16 = T("a16")
                xin0_ = car.take(D, F32)
                if car.off % 2:
                    car.off += 1
                off_x1 = car.off
                xin1_ = car.take(D, F32)
                xin = [xin0_, xin1_]; t_xin = [T("xin0"), T("xin1")]
                wppb = arena[:, off_x1:off_x1 + 2 * D]
                d_wpp = S.new_dma_slot() if s == 0 else d_wpp
                ost = [car.take(D, F32) for _ in range(2)]; t_ost = [T("ost0"), T("ost1")]
                tmpf = [car.take(TT, F32) for _ in range(3)]; t_tmp = [T("tmp%d" % i) for i in range(3)]
                rstd = car.take(TT, F32); t_rstd = T("rstd")
                mean_sb = tmpf[0]; nmr = tmpf[1]; vtmp = tmpf[2]
                pin = car.take(4 * PLE, F32); t_pin = T("pin")
                pbb = car.take(4 * PLE, BF16); t_pbb = T("pbb")
                pT = car.take(2 * TT, BF16); t_pT = T("pT")
                tmi = [0]
                bki = [0]

                def nbank():
                    bki[0] += 1
                    return bki[0] % 6

                def layer_norm(gk, bk, final):
                    y4 = lambda g4: y[:, 4 * g4 * TT:(4 * g4 + 4) * TT]
                    for g4 in range(4):
                        cs = list(range(4 * g4, 4 * g4 + 4))
                        cp("dve", hb[:, 4 * g4 * TT:(4 * g4 + 4) * TT], y4(g4), [t_y[c] for c in cs], [t_hb[c] for c in cs])
                        act(ab[:, 4 * g4 * TT:(4 * g4 + 4) * TT], y4(g4), AF.Square, [t_y[c] for c in cs], [t_ab[min(c, NA - 1)] for c in cs] + [t_ab16])
                    be = nbank()
                    for c in range(DC):
                        mm(PB[6][:, :], t_PB[6], meanm, hbv(c), c == 0, c == DC - 1, [t_const, t_hb[c]])
                    for c in range(DC):
                        mm(PB[be][:, :], t_PB[be], meanm, ab[:, c * TT:(c + 1) * TT], c == 0, c == DC - 1, [t_const, t_ab[min(c, NA - 1)], t_ab16])
                    t_st = t_rstd
                    cp("act", mean_sb, PB[6][:, :], [t_PB[6], t_tmp[0]], [t_st, t_tmp[0]])
                    tt("dve", vtmp, mean_sb, mean_sb, ALU.mult, [t_st, t_tmp[2]], [t_st, t_tmp[2]])
                    tt("dve", vtmp, PB[be][:, :], vtmp, ALU.subtract, [t_PB[be], t_st], [t_st, t_tmp[2]])
                    ts("dve", vtmp, vtmp, 0.0, EPS, ALU.max, ALU.add, [t_st], [t_st, t_tmp[2]])
                    act(rstd, vtmp, AF.Ln, [t_st], [t_st])
                    act(rstd, rstd, AF.Exp, [t_st], [t_st], scale=-0.5)
                    stt(nmr, mean_sb, -1.0, rstd, ALU.mult, ALU.mult, [t_st, t_tmp[1]], [t_st, t_tmp[1]])
                    rs3 = rstd.unsqueeze(1).broadcast_to([128, 4, TT])
                    nm3 = nmr.unsqueeze(1).broadcast_to([128, 4, TT])
                    for g4 in range(4):
                        cs = list(range(4 * g4, 4 * g4 + 4))
                        y43 = y4(g4).rearrange("p (a t) -> p a t", t=TT)
                        tt("dve", y43, y43, rs3, ALU.mult, [t_y[c] for c in cs] + [t_st], [t_y[c] for c in cs])
                        tt("dve", y43, y43, nm3, ALU.add, [t_y[c] for c in cs] + [t_st], [t_y[c] for c in cs])
                        for c in cs:
                            if final:
                                act(yv(c), yv(c), AF.Identity, [t_y[c], tc_], [t_y[c]], bias=LNP(bk, c), scale=LNP(gk, c))
                            else:
                                act(hbv(c), yv(c), AF.Identity, [t_y[c], tc_], [t_hb[c]], bias=LNP(bk, c), scale=LNP(gk, c))
                                act(yv(c), yv(c), AF.Identity, [t_y[c], tc_], [t_y[c]], bias=GA_(bk, c), scale=GA_(gk, c))

                xpre = {}

                def issue_xload(ttok_, b_):
                    if (ttok_, b_) in xpre:
                        return
                    xpre[(ttok_, b_)] = True
                    sl_ = b_ % 2
                    S.op("sp", lambda e, sl_=sl_, b_=b_, ttok_=ttok_, xin=xin: e.dma_start(out=xin[sl_], in_=X[ttok_ + b_ * 128:ttok_ + (b_ + 1) * 128, :]),
                         writes=[t_xin[sl_]], dma=True, dma_slot=d_xin[sl_])

                for t in range(SEQ // TT):
                    ttok = tok0 + t * TT
                    for b in range(4):
                        sl = b % 2
                        issue_xload(ttok, b)
                        for g in range(4):
                            bk = nbank()
                            for a in range(4):
                                c = 4 * g + a
                                S.op("pe", lambda e, bk=bk, a=a, c=c, sl=sl, xin=xin: e.transpose(PB[bk][:, a * 128:(a + 1) * 128],
                                                                                                  xin[sl][:, c * 128:(c + 1) * 128], identF[:]),
                                     reads=[t_xin[sl], t_const], writes=[t_PB[bk]])
                            dst = y[:, 4 * g * TT:(4 * g + 4) * TT].rearrange("p (a t) -> p a t", t=TT)[:, :, b * 128:(b + 1) * 128]
                            src = PB[bk][:, :].rearrange("p (a t) -> p a t", t=128)
                            ts("dve", dst, src, ALPHA, None, ALU.mult, None, [t_PB[bk]], [t_y[4 * g + a_] for a_ in range(4)])
                    for c in range(DC):
                        sl = load_w(Wo[:, :, c * 128:(c + 1) * 128], DC)
                        bk = nbank()
                        for e_ in range(DC):
                            mm(PB[bk][:, :], t_PB[bk], wk(sl, e_), mview(e_, t * TT, TT), e_ == 0, e_ == DC - 1, [t_wsl[sl], mT[e_][t]])
                        tt("dve", yv(c), yv(c), PB[bk][:, :], ALU.add, [t_y[c], t_PB[bk]], [t_y[c]])
                    layer_norm(0, 1, False)
                    for (j0, nj) in ((0, 15), (15, 15), (30, 14)):
                        for jj in range(nj):
                            j = j0 + jj
                            slg = load_w(Wg[:, :, j * 128:(j + 1) * 128], DC)
                            slu = load_w(Wu[:, :, j * 128:(j + 1) * 128], DC)
                            bg = nbank(); bu = nbank()
                            for d in range(DC):
                                mm(PB[bg][:, :], t_PB[bg], wk(slg, d), hbv(d), d == 0, d == DC - 1, [t_wsl[slg], t_hb[d]])
                            for d in range(DC):
                                mm(PB[bu][:, :], t_PB[bu], wk(slu, d), hbv(d), d == 0, d == DC - 1, [t_wsl[slu], t_hb[d]])
                            k_ = tmi[0] % 3; tmi[0] += 1
                            act(tmpf[k_], PB[bg][:, :], AF.Silu, [t_PB[bg]], [t_tmp[k_]])
                            tt("dve", abv(jj), tmpf[k_], PB[bu][:, :], ALU.mult, [t_tmp[k_], t_PB[bu]], [t_ab[jj]])
                        for c in range(DC):
                            sl = load_w(Wd[:, j0:j0 + nj, c * 128:(c + 1) * 128], nj)
                            bk = nbank()
                            for jj in range(nj):
                                mm(PB[bk][:, :], t_PB[bk], wk(sl, jj), abv(jj), jj == 0, jj == nj - 1, [t_wsl[sl], t_ab[jj]])
                            tt("dve", yv(c), yv(c), PB[bk][:, :], ALU.add, [t_y[c], t_PB[bk]], [t_y[c]])
                    layer_norm(2, 3, False)
                    for b in range(4):
                        S.op("sp", lambda e, b=b, ttok=ttok, pin=pin: e.dma_start(out=pin[:, b * PLE:(b + 1) * PLE], in_=P_[ttok + b * 128:ttok + (b + 1) * 128, :]),
                             writes=[t_pin], dma=True, dma_slot=d_p)
                    cp("dve", pbb, pin, [t_pin], [t_pbb])
                    for b in range(4):
                        for k2 in range(2):
                            S.op("pe", lambda e, b=b, k2=k2: e.transpose(PBF[0][:, k2 * 128:(k2 + 1) * 128],
                                                                         pbb[:, b * PLE + k2 * 128:b * PLE + (k2 + 1) * 128], identB),
                                 reads=[t_pbb, t_const], writes=[t_PBF[0]])
                        dst = pT.rearrange("p (k t) -> p k t", t=TT)[:, :, b * 128:(b + 1) * 128]
                        evac(dst, PBF[0][:, 0:256].rearrange("p (k t) -> p k t", t=128), [t_PBF[0]], [t_pT])
                    for c in range(DC):
                        if c == 0:
                            S.op("pool", lambda e, wppb=wppb: e.dma_start(out=wppb.rearrange("p (k c) -> p k c", c=D), in_=Wpp[:, :, :]),
                                 writes=[t_xin[1]], dma=True, dma_slot=d_wpp)
                        slg = load_w(Wpg[:, :, c * 128:(c + 1) * 128], DC)
                        bg = nbank(); be = nbank()
                        for d in range(DC):
                            mm(PB[bg][:, :], t_PB[bg], wk(slg, d), hbv(d), d == 0, d == DC - 1, [t_wsl[slg], t_hb[d]])
                        for k2 in range(2):
                            wpp_ = wppb[:, k2 * D + c * 128:k2 * D + (c + 1) * 128]
                            mm(PB[be][:, :], t_PB[be], wpp_, pT[:, k2 * TT:(k2 + 1) * TT], k2 == 0, k2 == 1, [t_xin[1], t_pT])
                        k_ = tmi[0] % 3; tmi[0] += 1
                        act(tmpf[k_], PB[bg][:, :], AF.Sigmoid, [t_PB[bg], tc_], [t_tmp[k_]], bias=LNP(6, c))
                        tt("dve", tmpf[k_], tmpf[k_], PB[be][:, :], ALU.mult, [t_tmp[k_], t_PB[be]], [t_tmp[k_]])
                        tt("dve", yv(c), yv(c), tmpf[k_], ALU.add, [t_y[c], t_tmp[k_]], [t_y[c]])
                    if t + 1 < SEQ // TT:
                        issue_xload(ttok + TT, 0)
                        issue_xload(ttok + TT, 1)
                    layer_norm(4, 5, True)
                    for b in range(4):
                        sl = b % 2
                        for g in range(4):
                            bk = nbank()
                            for a in range(4):
                                c = 4 * g + a
                                S.op("pe", lambda e, bk=bk, a=a, src_=yv(c)[:, b * 128:(b + 1) * 128]: e.transpose(PB[bk][:, a * 128:(a + 1) * 128], src_, identF[:]),
                                     reads=[t_y[c], t_const], writes=[t_PB[bk]])
                            evac(ost[sl][:, g * 512:(g + 1) * 512], PB[bk][:, :], [t_PB[bk]], [t_ost[sl]])
                        S.op("sp", lambda e, sl=sl, b=b, ttok=ttok, ost=ost: e.dma_start(out=OUT[ttok + b * 128:ttok + (b + 1) * 128, :], in_=ost[sl]),
                             reads=[t_ost[sl]], dma=True, dma_slot=d_out[sl])
                S.barrier()
        except _Stop:
            if debug:
                S.barrier()
                for e_ in range(DC):
                    S.op("pool", lambda e, e_=e_: e.dma_start(out=DBG[:, e_ * SEQ:(e_ + 1) * SEQ], in_=mergedT[:, e_ * SEQ:(e_ + 1) * SEQ]),
                         reads=[mT[e_][i] for i in range(4)], dma=True, dma_slot=S.new_dma_slot())
        S.emit()
    return nc


def _host_inputs(inp, ncores=N_CORES, nseq=SPC):
    f = lambda a: np.ascontiguousarray(np.asarray(a, dtype=np.float32))
    cm, oh, mrow = _consts()
    chunkT = lambda v, nch: f(np.asarray(v).reshape(nch, 128).T)
    pvec = np.zeros((128, 176), np.float32)
    cw = np.asarray(inp["conv_w"])[0]
    pvec[:, 0:32] = cw.reshape(4, 8, 128).transpose(2, 1, 0).reshape(128, 32)
    pvec[:, 32:40] = chunkT(inp["conv_b"][0], 8)
    pvec[:, 40:48] = chunkT(inp["lru_ba"][0], 8)
    pvec[:, 48:56] = chunkT(inp["lru_bx"][0], 8)
    pvec[:, 56:64] = chunkT(inp["lru_lambda"][0], 8)
    for k, nm in enumerate(("ln1_g", "ln1_b", "ln2_g", "ln2_b", "ln3_g", "ln3_b", "b_ple_gate")):
        pvec[:, 64 + 16 * k:80 + 16 * k] = chunkT(inp[nm][0], 16)
    brow = np.zeros((1, 640), np.float32)
    brow[0, 0:64] = inp["diff_lq1"][0]
    brow[0, 64:128] = inp["diff_lk1"][0]
    brow[0, 128:192] = inp["diff_lq2"][0]
    brow[0, 192:256] = inp["diff_lk2"][0]
    brow[0, 256:384] = inp["diff_subln_g"][0]
    brow[0, 384:640] = np.asarray(inp["rel_bias"]).reshape(-1)
    lw = np.stack([np.asarray(inp["lru_wa"])[0], np.asarray(inp["lru_wx"])[0]], 0)
    lruw = f(lw.transpose(2, 0, 1, 3).reshape(128, 2 * 8 * 128))
    shared = {
        "w_in": f(inp["w_in"][0]), "w_out": f(inp["w_out"][0]), "w_gate": f(inp["w_ffn_gate"][0]),
        "w_up": f(inp["w_ffn_up"][0]), "w_down": f(inp["w_ffn_down"][0]), "w_pgate": f(inp["w_ple_gate"][0]),
        "w_pproj": f(inp["w_ple_proj"][0]), "lruw": lruw, "pvec": pvec, "brow": brow,
        "relb": f(inp["rel_bias"]), "cmat": cm, "oneh": oh, "mrow": mrow,
    }
    x = np.asarray(inp["x"], dtype=np.float32)
    p = np.asarray(inp["p"], dtype=np.float32)[0]
    maps = []
    for c in range(ncores):
        m = dict(shared)
        m["x"] = f(x[c * nseq:(c + 1) * nseq].reshape(nseq * SEQ, D))
        m["p"] = f(p[c * nseq:(c + 1) * nseq].reshape(nseq * SEQ, PLE))
        maps.append(m)
    return maps


def kernel(**inputs):
    nc = build()
    maps = _host_inputs(inputs)
    res = run_bass_kernel_spmd(nc, maps, core_ids=list(range(N_CORES)))
    outs = [np.asarray(r["out"]).reshape(SPC, SEQ, D) for r in res.results]
    return np.concatenate(outs, axis=0).astype(np.float32)
```

```python
import contextlib
import math
import numpy as np
import concourse.bass as bass
import concourse.mybir as mybir
from concourse.bass_utils import run_bass_kernel_spmd

F32 = mybir.dt.float32
BF16 = mybir.dt.bfloat16
AF = mybir.ActivationFunctionType
ALU = mybir.AluOpType
AX = mybir.AxisListType

N_CORES = 8
D = 2048
SEQ = 2048
BATCH = 16
SPC = BATCH // N_CORES
DC = D // 128
IN_W = 5120
FFN = 5632
FC = FFN // 128
PLE = 256
NTB = SEQ // 128
ALPHA = 2.0 ** 0.25
LAM_INIT = 0.2
EPS = 1e-5
LRU_C = 8.0
NEG = -30000.0

ENGS = ("pe", "act", "dve", "pool", "sp")


class T:
    __slots__ = ("name", "w", "r", "rd")

    def __init__(self, name=""):
        self.name = name
        self.w = None
        self.r = {}
        self.rd = []


class Op:
    __slots__ = ("eng", "fn", "deps", "idx", "signal", "sigval", "dma", "sem", "semval", "ndma")

    def __init__(self, eng, fn, dma=False, ndma=1):
        self.eng = eng
        self.fn = fn
        self.deps = []
        self.signal = False
        self.sigval = 0
        self.dma = dma
        self.sem = None
        self.semval = 0
        self.ndma = ndma
        self.idx = 0


class Sched:
    def __init__(self, nc):
        self.nc = nc
        self.ops = {e: [] for e in ENGS}
        self.n_slots = 0
        self.all_ops = []
        self.pending = {}
        self.last_dma = {}

    def op(self, eng, fn, reads=(), writes=(), dma=False, dma_slot=None, ndma=1):
        o = Op(eng, fn, dma=dma, ndma=ndma)
        o.idx = len(self.ops[eng])
        best = {}
        dd = []

        def add(d):
            if d is None or d is o:
                return
            if d.dma:
                if d not in dd:
                    dd.append(d)
            else:
                b = best.get(d.eng)
                if b is None or d.idx > b.idx:
                    best[d.eng] = d

        for t in reads:
            add(t.w)
        for t in writes:
            add(t.w)
            for x in t.r.values():
                add(x)
            for x in t.rd:
                add(x)
        for x in self.pending.pop(eng, ()):
            add(x)
        o.deps = list(best.values()) + dd
        for t in reads:
            if dma:
                t.rd.append(o)
            else:
                t.r[eng] = o
        for t in writes:
            t.w = o
            t.r = {}
            t.rd = []
        if dma:
            o.sem = dma_slot
            self.last_dma[dma_slot] = o
        self.ops[eng].append(o)
        self.all_ops.append(o)
        return o

    def new_dma_slot(self):
        self.n_slots += 1
        return self.n_slots - 1

    def barrier(self):
        lasts = []
        for e in ENGS:
            for o in reversed(self.ops[e]):
                if not o.dma:
                    lasts.append(o)
                    break
        lasts.extend(self.last_dma.values())
        for e in ENGS:
            self.pending[e] = list(lasts) + list(self.pending.get(e, ()))

    def emit(self):
        nc = self.nc
        for o in self.all_ops:
            for d in o.deps:
                if (not d.dma) and d.eng == o.eng and o.eng == "pe":
                    continue
                d.signal = True
        for e in ENGS:
            c = 0
            for o in self.ops[e]:
                if o.dma:
                    continue
                if o.signal:
                    c += 1
                    o.sigval = c
        cnt = [0] * self.n_slots
        for o in self.all_ops:
            if o.dma:
                cnt[o.sem] += 16 * o.ndma
                o.semval = cnt[o.sem]
        with contextlib.ExitStack() as st:
            eng_sem = {e: st.enter_context(nc.semaphore("prog_" + e)) for e in ENGS}
            dsem = [st.enter_context(nc.semaphore("dma_%d" % i)) for i in range(self.n_slots)]
            block = st.enter_context(nc.Block())

            def run_engine(ename, eng):
                waited = {}
                for o in self.ops[ename]:
                    for d in o.deps:
                        if d.dma:
                            key = ("d", d.sem)
                            s, v = dsem[d.sem], d.semval
                        else:
                            if d.eng == ename and ename == "pe":
                                continue
                            key = ("e", d.eng)
                            s, v = eng_sem[d.eng], d.sigval
                        if waited.get(key, 0) >= v:
                            continue
                        waited[key] = v
                        eng.wait_ge(s, v)
                    ins = o.fn(eng)
                    if o.dma:
                        if not isinstance(ins, (list, tuple)):
                            ins = [ins]
                        assert len(ins) == o.ndma
                        for i_ in ins:
                            i_.then_inc(dsem[o.sem], 16)
                    elif o.signal:
                        ins.then_inc(eng_sem[ename], 1)
                for o in self.ops[ename]:
                    if o.dma:
                        key = ("d", o.sem)
                        if waited.get(key, 0) < o.semval:
                            waited[key] = o.semval
                            eng.wait_ge(dsem[o.sem], o.semval)

            @block.tensor
            def _(eng):
                run_engine("pe", eng)

            @block.scalar
            def _(eng):
                run_engine("act", eng)

            @block.vector
            def _(eng):
                run_engine("dve", eng)

            @block.gpsimd
            def _(eng):
                run_engine("pool", eng)

            @block.sync
            def _(eng):
                run_engine("sp", eng)


def _bucket(n):
    max_exact = 16
    nf = np.maximum(n, 1).astype(np.float32)
    large = max_exact + (np.log(nf / np.float32(max_exact)) / np.float32(math.log(128 / max_exact))
                         * np.float32(32 - max_exact)).astype(np.int32)
    large = np.minimum(large, 31)
    return np.where(n < max_exact, n, large)


def _consts():
    cm = np.zeros((128, 6 * 128), np.float32)
    cm[:, 0:128] = np.eye(128)
    cm[:, 128:256] = np.eye(128)[::-1]
    cm[0:64, 256:384] = 1.0
    cm[64:128, 384:512] = 1.0
    cm[:, 512:640] = 1.0 / D
    cm[:, 640:768] = 1.0
    oh = np.zeros((32, 384), np.float32)
    mrow = np.zeros((8, 384), np.float32)
    for m in range(384):
        n = m - 127
        if n < 0:
            mrow[:, m] = NEG
        else:
            b = int(_bucket(np.array([n]))[0])
            oh[b, m] = 1.0
    return cm, oh, mrow


class _Stop(Exception):
    pass


def build(nseq=SPC, debug=False, stop=None):
    nc = bass.Bass("TRN2", target_bir_lowering=False)
    NTOK = nseq * SEQ
    dram = lambda name, shape, kind="ExternalInput": nc.dram_tensor(name, shape, F32, kind=kind).ap()
    X = dram("x", [NTOK, D])
    P_ = dram("p", [NTOK, PLE])
    W_IN = dram("w_in", [D, IN_W])
    W_OUT = dram("w_out", [D, D])
    W_G = dram("w_gate", [D, FFN])
    W_U = dram("w_up", [D, FFN])
    W_D = dram("w_down", [FFN, D])
    W_PG = dram("w_pgate", [D, D])
    W_PP = dram("w_pproj", [PLE, D])
    LRUW = dram("lruw", [128, 2 * 8 * 128])
    PVEC = dram("pvec", [128, 176])
    BROW = dram("brow", [1, 640])
    RELB = dram("relb", [32, 8])
    CMAT = dram("cmat", [128, 768])
    ONEH = dram("oneh", [32, 384])
    MROW = dram("mrow", [8, 384])
    OUT = dram("out", [NTOK, D], kind="ExternalOutput")
    SCR = dram("scr", [8, 384], kind="Internal")
    DBG = dram("dbg", [128, 16 * SEQ], kind="ExternalOutput") if debug else None

    S = Sched(nc)
    with contextlib.ExitStack() as st:
        sbt = lambda name, shape, dt=F32: st.enter_context(nc.sbuf_tensor(name, shape, dt))
        identF = sbt("identF", [128, 128], F32)
        cb = sbt("cb", [128, 768], BF16)
        identB, Jb, sel0, sel1, meanm, onesb = (cb[:, i * 128:(i + 1) * 128] for i in range(6))
        pv = sbt("pv", [128, 176], F32)
        der = sbt("der", [128, 96], F32)
        lruw = sbt("lruw_sb", [128, 2 * 8 * 128], BF16)
        HB = sbt("HB", [128, 8 * 256], BF16)
        gsub = sbt("gsub", [128, 128], F32)
        sm = sbt("sm", [128, 64], F32)
        mergedT = sbt("mergedT", [128, DC * SEQ], BF16)
        NSLOT = 4
        wsl = [sbt("wslot%d" % i, [128, 16 * 128], BF16) for i in range(NSLOT)]
        import os
        ARENA = int(os.environ.get('ARENA', '59400'))
        arena = sbt("arena", [128, ARENA], BF16)

        t_const = T("const")
        t_wsl = [T("wsl%d" % i) for i in range(NSLOT)]
        d_wsl = [S.new_dma_slot() for _ in range(NSLOT)]
        mT = [[T("mT%d_%d" % (e, tt)) for tt in range(4)] for e in range(DC)]

        PB = [st.enter_context(nc.psum_tensor("pb%d" % i, [128, 512], F32)) for i in range(7)]
        t_PB = [T("pb%d" % i) for i in range(7)]
        pbf_all = st.enter_context(nc.psum_tensor("pbf0", [128, 1024], BF16))
        PBF = [pbf_all[:, 0:512]]
        t_PBF = [T("pbf0")]

        class Carver:
            def __init__(self):
                self.off = 0

            def take(self, nelem, dt):
                if dt == F32:
                    if self.off % 2:
                        self.off += 1
                    a = arena[:, self.off:self.off + 2 * nelem].bitcast(F32)
                    self.off += 2 * nelem
                else:
                    a = arena[:, self.off:self.off + nelem]
                    self.off += nelem
                assert self.off <= ARENA, self.off
                return a

        def dma(q, out, in_, reads=(), writes=(), slot=None, allow=False):
            if slot is None:
                slot = S.new_dma_slot()
            if allow:
                f = lambda e: e.dma_start(out=out, in_=in_, allow_slow_non_contiguous=True)
            else:
                f = lambda e: e.dma_start(out=out, in_=in_)
            return S.op(q, f, reads=reads, writes=writes, dma=True, dma_slot=slot)

        wi = [0]

        def load_w(src_ap, nk):
            s = wi[0] % NSLOT
            wi[0] += 1
            dst = wsl[s][:, 0:nk * 128].rearrange("p (k c) -> p k c", c=128)
            S.op("pool", lambda e: e.dma_start(out=dst, in_=src_ap), writes=[t_wsl[s]], dma=True, dma_slot=d_wsl[s])
            return s

        def wk(s, k):
            return wsl[s][:, k * 128:(k + 1) * 128]

        def mm(ps_ap, t_ps, lhsT, rhs, start, stop, reads):
            return S.op("pe", lambda e: e.matmul(ps_ap, lhsT=lhsT, rhs=rhs, start=start, stop=stop),
                        reads=reads, writes=[t_ps])

        def act(out, in_, func, reads, writes, bias=None, scale=None, accum_out=None):
            kw = {}
            if bias is not None:
                kw["bias"] = bias
            if scale is not None:
                kw["scale"] = scale
            if accum_out is not None:
                kw["accum_out"] = accum_out
            return S.op("act", lambda e: e.activation(out=out, in_=in_, func=func, **kw), reads=reads, writes=writes)

        def ts(eng, out, in0, s1, s2, op0, op1, reads, writes):
            if s2 is None:
                return S.op(eng, lambda e: e.tensor_scalar(out=out, in0=in0, scalar1=s1, scalar2=None, op0=op0),
                            reads=reads, writes=writes)
            return S.op(eng, lambda e: e.tensor_scalar(out=out, in0=in0, scalar1=s1, scalar2=s2, op0=op0, op1=op1),
                        reads=reads, writes=writes)

        def tt(eng, out, in0, in1, op, reads, writes):
            return S.op(eng, lambda e: e.tensor_tensor(out=out, in0=in0, in1=in1, op=op), reads=reads, writes=writes)

        def stt(out, in0, scalar, in1, op0, op1, reads, writes):
            return S.op("dve", lambda e: e.scalar_tensor_tensor(out=out, in0=in0, scalar=scalar, in1=in1, op0=op0, op1=op1),
                        reads=reads, writes=writes)

        def cp(eng, out, in_, reads, writes):
            if eng == "act":
                return act(out, in_, AF.Identity, reads, writes)
            return S.op(eng, lambda e: e.tensor_copy(out=out, in_=in_), reads=reads, writes=writes)

        evi = [0]

        def evac(out, in_, reads, writes):
            evi[0] += 1
            import os
            if os.environ.get("EVAC_DVE"):
                return cp("dve", out, in_, reads, writes)
            return cp("act" if evi[0] % 2 else "dve", out, in_, reads, writes)

        try:
            car0 = Carver()
            bc = car0.take(640, F32)
            dma("sp", identF[:], CMAT[:, 0:128], writes=[t_const])
            dma("pool", cb[:], CMAT[:, 0:768], writes=[t_const])
            dma("sp", pv[:], PVEC[:, :], writes=[t_const])
            dma("sp", bc, BROW.partition_broadcast(128), writes=[t_const])
            dma("pool", lruw[:], LRUW[:, :], writes=[t_const])
            if stop == 'S1':
                raise _Stop()
            CW = lambda n, j: pv[:, n * 4 + j:n * 4 + j + 1]
            CBc = lambda n: pv[:, 32 + n:33 + n]
            BA = lambda n: pv[:, 40 + n:41 + n]
            BX = lambda n: pv[:, 48 + n:49 + n]
            LNP = lambda k, c: pv[:, 64 + k * 16 + c:64 + k * 16 + c + 1]
            LSC = lambda n: der[:, n:n + 1]
            LSC2 = lambda n: der[:, 8 + n:9 + n]
            GA_ = lambda k, c: der[:, 16 + k * 16 + c:16 + k * 16 + c + 1]
            NEGLAM = der[:, 80:81]
            EPSC = der[:, 81:82]
            ONEC = der[:, 82:83]
            B8S = lambda h: der[:, 84 + h:85 + h]
            GCOL = der[:, 83:84]
            tc_ = t_const
            S.op("dve", lambda e: e.memset(der[:, 81:82], EPS), writes=[tc_])
            S.op("dve", lambda e: e.memset(der[:, 82:83], 1.0), writes=[tc_])
            ts("dve", der[:, 16:80], pv[:, 64:128], ALPHA, None, ALU.mult, None, [tc_], [tc_])
            L = pv[:, 56:64]
            m_ = sm[:, 0:8]; z_ = sm[:, 8:16]; w_ = sm[:, 16:24]; w2 = sm[:, 24:32]; pl = sm[:, 32:40]
            ts("dve", m_, L, -1.0, None, ALU.mult, None, [tc_], [tc_])
            tt("dve", z_, m_, L, ALU.max, [tc_], [tc_])
            ts("dve", m_, m_, 0.0, None, ALU.max, None, [tc_], [tc_])
            act(z_, z_, AF.Exp, [tc_], [tc_], scale=-1.0)
            ts("dve", w_, z_, 2.0, None, ALU.add, None, [tc_], [tc_])
            S.op("dve", lambda e: e.reciprocal(out=w_, in_=w_), reads=[tc_], writes=[tc_])
            tt("dve", w_, w_, z_, ALU.mult, [tc_], [tc_])
            tt("dve", w2, w_, w_, ALU.mult, [tc_], [tc_])
            ts("dve", pl, w2, 1.0 / 9.0, 1.0 / 7.0, ALU.mult, ALU.add, [tc_], [tc_])
            for cf in (1.0 / 5.0, 1.0 / 3.0, 1.0):
                tt("dve", pl, pl, w2, ALU.mult, [tc_], [tc_])
                ts("dve", pl, pl, cf, None, ALU.add, None, [tc_], [tc_])
            tt("dve", pl, pl, w_, ALU.mult, [tc_], [tc_])
            ts("dve", pl, pl, 2.0, None, ALU.mult, None, [tc_], [tc_])
            tt("dve", pl, pl, m_, ALU.add, [tc_], [tc_])
            ts("dve", der[:, 0:8], pl, -LRU_C, None, ALU.mult, None, [tc_], [tc_])
            ts("dve", der[:, 8:16], pl, -2.0 * LRU_C, None, ALU.mult, None, [tc_], [tc_])
            if stop == 'S2':
                raise _Stop()
            pr = sm[:, 40:42]
            tmpl = sm[:, 42:44]
            for k in range(2):
                S.op("dve", lambda e, k=k: e.tensor_tensor(out=gsub[:, 0:64], in0=bc[:, k * 128:k * 128 + 64],
                                                           in1=bc[:, k * 128 + 64:k * 128 + 128], op=ALU.mult),
                     reads=[tc_], writes=[tc_])
                S.op("dve", lambda e, k=k: e.reduce_sum(out=pr[:, k:k + 1], in_=gsub[:, 0:64], axis=AX.X), reads=[tc_], writes=[tc_])
            act(pr, pr, AF.Exp, [tc_], [tc_])
            tt("dve", tmpl[:, 0:1], pr[:, 1:2], pr[:, 0:1], ALU.subtract, [tc_], [tc_])
            ts("dve", der[:, 80:81], tmpl[:, 0:1], -LAM_INIT, None, ALU.add, None, [tc_], [tc_])
            dma("sp", der[:, 83:84], BROW[0:1, 256:384].rearrange("a d -> d a"), writes=[tc_], allow=True)
            ts("dve", der[:, 83:84], der[:, 83:84], 1.0 - LAM_INIT, None, ALU.mult, None, [tc_], [tc_])
            ts("dve", gsub[:], bc[:, 256:384], 1.0 - LAM_INIT, None, ALU.mult, None, [tc_], [tc_])
            bm = sm[:, 44:52]
            S.op("dve", lambda e: e.reduce_max(out=bm, in_=bc[:, 384:640].rearrange("p (b h) -> p h b", h=8), axis=AX.X),
                 reads=[tc_], writes=[tc_])
            tt("dve", bm, bm, bc[:, 384 + 248:384 + 256], ALU.subtract, [tc_], [tc_])
            ts("dve", der[:, 84:92], bm, 0.0, None, ALU.max, None, [tc_], [tc_])
            if stop == 'S3':
                raise _Stop()
            relb_sb = car0.take(8, F32)[0:32, :]
            oneh_sb = car0.take(384, F32)[0:32, :]
            mrow_sb = car0.take(384, F32)[0:8, :]
            c8 = car0.take(1, F32)[0:8, :]
            vec8 = car0.take(384, F32)[0:8, :]
            hst = car0.take(256, F32)
            t_set = T("setup")
            dma("sp", relb_sb, RELB[:, :], writes=[t_set])
            dma("sp", oneh_sb, ONEH[:, :], writes=[t_set])
            dma("sp", mrow_sb, MROW[:, :], writes=[t_set])
            dma("sp", c8, RELB[31:32, :].rearrange("a h -> h a"), writes=[t_set], allow=True)
            mm(PB[0][0:8, 0:384], t_PB[0], relb_sb, oneh_sb, True, True, [t_set])
            ts("dve", vec8, PB[0][0:8, 0:384], c8, 8.0, ALU.subtract, ALU.mult, [t_PB[0], t_set], [t_set])
            tt("dve", vec8, vec8, mrow_sb, ALU.add, [t_set], [t_set])
            if stop == 'S4':
                raise _Stop()
            t_scr = T("scr")
            dma("sp", SCR[:, :], vec8, reads=[t_set], writes=[t_scr])
            for h in range(8):
                hk = bass.AP(tensor=SCR.tensor, offset=h * 384, ap=[[1, 128], [1, 256]])
                dma("sp", hst, hk, reads=[t_scr], writes=[t_set])
                cp("dve", HB[:, h * 256:(h + 1) * 256], hst, [t_set], [tc_])
            S.barrier()
            if stop == 'setup':
                raise _Stop()

            Wv = W_IN.rearrange("(c p) e -> p c e", p=128)
            Wo = W_OUT.rearrange("(c p) e -> p c e", p=128)
            Wg = W_G.rearrange("(c p) e -> p c e", p=128)
            Wu = W_U.rearrange("(c p) e -> p c e", p=128)
            Wd = W_D.rearrange("(c p) e -> p c e", p=128)
            Wpg = W_PG.rearrange("(c p) e -> p c e", p=128)
            Wpp = W_PP.rearrange("(c p) e -> p c e", p=128)
            mview = lambda e, c0, n: mergedT[:, e * SEQ + c0:e * SEQ + c0 + n]
            d_xin = [S.new_dma_slot() for _ in range(2)]
            d_xbf = [S.new_dma_slot() for _ in range(4)]
            d_out = [S.new_dma_slot() for _ in range(2)]
            d_p = S.new_dma_slot()

            for s in range(nseq):
                tok0 = s * SEQ
                car = Carver()
                xT = car.take(DC * SEQ, BF16)
                xTv = lambda d, c0, n: xT[:, d * SEQ + c0:d * SEQ + c0 + n]
                t_xT = [T("xT%d" % i) for i in range(4)]
                mark = car.off
                NXB = 4
                xbf = [car.take(D, BF16) for _ in range(NXB)]
                t_xbf = [T("xbf%d" % i) for i in range(NXB)]
                abanks = [(PB[k_][:, 0:256].bitcast(BF16), t_PB[k_]) for k_ in range(7)] + [(PBF[0], t_PBF[0])]
                gi_ = 0
                for blk in range(NTB):
                    sl = blk % NXB
                    S.op("pool", lambda e, sl=sl, blk=blk, tok0=tok0, xbf=xbf: [e.dma_start(out=xbf[sl][:, hh * 1024:(hh + 1) * 1024],
                                                                        in_=X[tok0 + blk * 128:tok0 + (blk + 1) * 128, hh * 1024:(hh + 1) * 1024])
                                                            for hh in range(2)],
                         writes=[t_xbf[sl]], dma=True, dma_slot=d_xbf[sl], ndma=2)
                    for g in range(4):
                        bank_ap, t_bank = abanks[gi_ % len(abanks)]
                        gi_ += 1
                        for a in range(4):
                            c = 4 * g + a
                            S.op("pe", lambda e, bank_ap=bank_ap, a=a, c=c, sl=sl, xbf=xbf: e.transpose(bank_ap[:, a * 128:(a + 1) * 128],
                                                                                                     xbf[sl][:, c * 128:(c + 1) * 128], identB),
                                 reads=[t_xbf[sl], t_const], writes=[t_bank])
                        dst = xT[:, 4 * g * SEQ:(4 * g + 4) * SEQ].rearrange("p (a t) -> p a t", t=SEQ)[:, :, blk * 128:(blk + 1) * 128]
                        src = bank_ap.rearrange("p (a t) -> p a t", t=128)
                        evac(dst, src, [t_bank], [t_xT[blk // 4]])
                S.barrier()
                car.off = mark
                if stop == 'A0':
                    raise _Stop()

                def inproj_mm(col0):
                    sl = load_w(Wv[:, :, col0:col0 + 128], DC)
                    for d in range(DC):
                        for t4 in range(4):
                            mm(PB[t4][:, :], t_PB[t4], wk(sl, d), xTv(d, t4 * 512, 512), d == 0, d == DC - 1,
                               [t_wsl[sl], t_xT[t4]])

                def inproj_evac(dst_fn, dst_tiles):
                    for t4 in range(4):
                        evac(dst_fn(t4), PB[t4][:, :], [t_PB[t4]], dst_tiles)

                xrp2 = [car.take(SEQ + 4, F32) for _ in range(2)]; t_xrp2 = [T("xrp0"), T("xrp1")]
                ygb2 = [car.take(SEQ, F32) for _ in range(2)]; t_ygb2 = [T("ygb0"), T("ygb1")]
                QW = 512
                LS = []
                for si in range(2):
                    LS.append(dict(conv=car.take(QW, F32), ga=car.take(QW, F32), gx=car.take(QW, F32), A=car.take(QW, F32),
                                   convb=car.take(QW, BF16), t_conv=T("conv%d" % si), t_ga=T("ga%d" % si), t_gx=T("gx%d" % si),
                                   t_A=T("A%d" % si), t_convb=T("convb%d" % si)))
                carry4 = sm[:, 24:28]; t_carry = [T("carry%d" % i) for i in range(4)]
                for b_ in range(2):
                    S.op("dve", lambda e, b_=b_: e.memset(xrp2[b_][:, 0:4], 0.0), writes=[t_xrp2[b_]])
                gbank = [0]

                def lru_math(n, q):
                    L_ = LS[q % 2]
                    conv_h, ga_h, gx_h, A_h, convb = L_["conv"], L_["ga"], L_["gx"], L_["A"], L_["convb"]
                    t_conv, t_ga, t_gx, t_A, t_convb = L_["t_conv"], L_["t_ga"], L_["t_gx"], L_["t_A"], L_["t_convb"]
                    xrp = xrp2[n % 2]; t_xrp = t_xrp2[n % 2]
                    ygb = ygb2[n % 2]; t_ygb = t_ygb2[n % 2]
                    c0 = q * QW
                    ts("dve", conv_h, xrp[:, c0 + 3:c0 + 3 + QW], CW(n, 3), CBc(n), ALU.mult, ALU.add, [t_xrp, tc_], [t_conv]); yield
                    for j in (2, 1, 0):
                        stt(conv_h, xrp[:, c0 + j:c0 + j + QW], CW(n, j), conv_h, ALU.mult, ALU.add, [t_xrp, t_conv, tc_], [t_conv]); yield
                    cp("act", convb, conv_h, [t_conv], [t_convb]); yield
                    for g, (dstb, t_d, bcol) in enumerate(((ga_h, t_ga, BA(n)), (gx_h, t_gx, BX(n)))):
                        pbk = 4 + (gbank[0] % 3); gbank[0] += 1
                        wg_ = lruw[:, (g * 8 + n) * 128:(g * 8 + n + 1) * 128]
                        mm(PB[pbk][:, :], t_PB[pbk], wg_, convb, True, True, [tc_, t_convb])
                        act(dstb, PB[pbk][:, :], AF.Sigmoid, [t_PB[pbk], tc_], [t_d], bias=bcol); yield
                    act(A_h, ga_h, AF.Exp, [t_ga, tc_], [t_A], scale=LSC(n)); yield
                    act(ga_h, ga_h, AF.Exp, [t_ga, tc_], [t_ga], scale=LSC2(n)); yield
                    act(ga_h, ga_h, AF.Ln, [t_ga, tc_], [t_ga], scale=-(1.0 - 1e-6), bias=ONEC); yield
                    act(ga_h, ga_h, AF.Exp, [t_ga], [t_ga], scale=0.5); yield
                    tt("dve", gx_h, gx_h, ga_h, ALU.mult, [t_gx, t_ga], [t_gx]); yield
                    tt("dve", gx_h, gx_h, conv_h, ALU.mult, [t_gx, t_conv], [t_gx]); yield
                    if q == 0:
                        init = 0.0; rd = [t_A, t_gx]
                    elif q % 2 == 1:
                        other = LS[(q - 1) % 2]
                        init = other["conv"][:, QW - 1:QW]; rd = [t_A, t_gx, other["t_conv"]]
                    else:
                        init = carry4[:, q - 1:q]; rd = [t_A, t_gx, t_carry[q - 1]]
                    S.op("dve", lambda e, init=init, conv_h=conv_h, A_h=A_h, gx_h=gx_h: e.tensor_tensor_scan(out=conv_h, data0=A_h, data1=gx_h, initial=init,
                                                                           op0=ALU.mult, op1=ALU.add),
                         reads=rd, writes=[t_conv]); yield
                    if q == 1:
                        cp("dve", carry4[:, q:q + 1], conv_h[:, QW - 1:QW], [t_conv], [t_carry[q]])
                    yield
                    yg = ygb[:, c0:c0 + QW]
                    act(ga_h, yg, AF.Gelu_apprx_tanh, [t_ygb, t_ga], [t_ga]); yield
                    tt("dve", mview(n, c0, QW), ga_h, conv_h, ALU.mult, [t_ga, t_conv], [mT[n][q]]); yield

                def inproj_units(col0, dst_fn, dst_tile):
                    hold = {}
                    res = []
                    for t4 in range(4):
                        def mmu(t4=t4):
                            if t4 == 0:
                                hold["sl"] = load_w(Wv[:, :, col0:col0 + 128], DC)
                            sl = hold["sl"]
                            for d in range(DC):
                                mm(PB[t4][:, :], t_PB[t4], wk(sl, d), xTv(d, t4 * 512, 512), d == 0, d == DC - 1, [t_wsl[sl], t_xT[t4]])

                        def evu(t4=t4):
                            evac(dst_fn(t4), PB[t4][:, :], [t_PB[t4]], [dst_tile])
                        res.append((mmu, evu))
                    return res

                def lru_pair(n, qa, qb, units):
                    ga_, gb_ = lru_math(n, qa), lru_math(n, qb)
                    alive = [ga_, gb_]
                    step = 0
                    sched_ = {}
                    for i_, (mmu, evu) in enumerate(units):
                        sched_.setdefault(1 + 4 * i_, []).append(mmu)
                        sched_.setdefault(1 + 4 * i_ + 7, []).append(evu)
                    while alive or any(k_ >= step for k_ in sched_):
                        for g_ in list(alive):
                            try:
                                next(g_)
                            except StopIteration:
                                alive.remove(g_)
                        for f_ in sched_.pop(step, []):
                            f_()
                        step += 1
                        if not alive and not sched_:
                            break

                xr_dst = lambda n: (lambda t4: xrp2[n % 2][:, 3 + t4 * 512:3 + (t4 + 1) * 512])
                yg_dst = lambda n: (lambda t4: ygb2[n % 2][:, t4 * 512:(t4 + 1) * 512])
                inproj_mm(0); inproj_evac(xr_dst(0), [t_xrp2[0]])
                inproj_mm(1024); inproj_evac(yg_dst(0), [t_ygb2[0]])
                for n in range(8):
                    ux = inproj_units((n + 1) * 128, xr_dst(n + 1), t_xrp2[(n + 1) % 2]) if n + 1 < 8 else []
                    uy = inproj_units(1024 + (n + 1) * 128, yg_dst(n + 1), t_ygb2[(n + 1) % 2]) if n + 1 < 8 else []
                    lru_pair(n, 0, 1, ux)
                    lru_pair(n, 2, 3, uy)
                S.barrier()
                car.off = mark
                if stop == 'A1':
                    raise _Stop()

                qT2 = [car.take(SEQ, BF16) for _ in range(2)]; t_q2 = [T("qT0"), T("qT1")]
                kT2 = [car.take(SEQ, BF16) for _ in range(2)]; t_k2 = [T("kT0"), T("kT1")]
                vT2 = [car.take(SEQ, BF16) for _ in range(2)]; t_v2 = [T("vT0"), T("vT1")]
                vaug2 = [car.take(NTB * 130, BF16) for _ in range(2)]; t_va2 = [T("va0"), T("va1")]
                vav2 = [v_.rearrange("p (j c) -> p j c", c=130) for v_ in vaug2]
                NET = 4
                ET = [car.take(512, BF16) for _ in range(NET)]; t_ET = [T("ET%d" % i) for i in range(NET)]
                tAB = [car.take(512, F32), car.take(512, F32)]
                rcp = car.take(512, F32)
                sqb = car.take(512, BF16)
                t_ep = T("ep")
                segidx = {}
                for Tq_ in range(4):
                    for c_ in range(2):
                        segidx[(Tq_, c_)] = len(segidx)
                mx = sm[:, 54:58]
                nrm = sm[:, 58:62]
                nsh2 = [sm[:, 20:22], sm[:, 22:24]]
                t_nsh2 = [T("nsh0"), T("nsh1")]
                t_sm = T("sm")
                for b_ in range(2):
                    S.op("dve", lambda e, b_=b_: e.memset(vav2[b_][:, :, 128:130], 1.0), writes=[t_va2[b_]])
                LA = 1
                BGB = 6
                deferred = []

                def prologue_units(h):
                    hp = h % 2
                    qT, kT, vT = qT2[hp], kT2[hp], vT2[hp]
                    t_q, t_k, t_v, t_va = t_q2[hp], t_k2[hp], t_v2[hp], t_va2[hp]
                    vav = vav2[hp]
                    nsh = nsh2[hp]; t_nsh = t_nsh2[hp]
                    units = []
                    for (col0, dst, t_dst) in ((2048 + h * 128, qT, t_q), (3072 + h * 128, kT, t_k), (4096 + h * 128, vT, t_v)):
                        hold = {}

                        def ld(col0=col0, hold=hold):
                            hold["sl"] = load_w(Wv[:, :, col0:col0 + 128], DC)
                        for t4 in range(4):
                            def u(t4=t4, dst=dst, t_dst=t_dst, hold=hold, ld=ld):
                                if t4 == 0:
                                    ld()
                                sl = hold["sl"]
                                for d in range(DC):
                                    mm(PB[BGB][:, :], t_PB[BGB], wk(sl, d), xTv(d, t4 * 512, 512), d == 0, d == DC - 1,
                                       [t_wsl[sl], t_xT[t4]])
                                cp("dve", dst[:, t4 * 512:(t4 + 1) * 512], PB[BGB][:, :], [t_PB[BGB]], [t_dst])
                            units.append(u)
                    for g in range(4):
                        def u(g=g):
                            for a in range(4):
                                blk = 4 * g + a
                                S.op("pe", lambda e, a=a, blk=blk: e.transpose(PBF[0][:, a * 128:(a + 1) * 128],
                                                                              vT[:, blk * 128:(blk + 1) * 128], identB),
                                     reads=[t_v, t_const], writes=[t_PBF[0]])
                            cp("dve", vav[:, 4 * g:4 * g + 4, 0:128], PBF[0][:, :].rearrange("p (a t) -> p a t", t=128), [t_PBF[0]], [t_va])
                        units.append(u)
                    sq = vT
                    for qi, (src, t_src) in enumerate(((qT, t_q), (kT, t_k))):
                        for c, sel in enumerate((sel0, sel1)):
                            def u(qi=qi, c=c, sel=sel, src=src, t_src=t_src):
                                if c == 0:
                                    act(sq, src, AF.Square, [t_src, t_v], [t_v])
                                for t4 in range(4):
                                    mm(PB[BGB][:, :], t_PB[BGB], sel, sq[:, t4 * 512:(t4 + 1) * 512], True, True, [t_v, t_const])
                                    S.op("dve", lambda e, t4=t4: e.reduce_max(out=mx[:, t4:t4 + 1], in_=PB[BGB][:, :], axis=AX.X),
                                         reads=[t_PB[BGB]], writes=[t_sm])
                                S.op("dve", lambda e, qi=qi, c=c: e.reduce_max(out=nrm[:, qi * 2 + c:qi * 2 + c + 1], in_=mx, axis=AX.X),
                                     reads=[t_sm], writes=[t_sm])
                            units.append(u)

                    def fin():
                        tt("dve", nsh, nrm[:, 0:2], nrm[:, 2:4], ALU.mult, [t_sm], [t_nsh])
                        ts("dve", nsh, nsh, 1e-30, None, ALU.max, None, [t_nsh], [t_nsh])
                        act(nsh, nsh, AF.Ln, [t_nsh], [t_nsh])
                        act(nsh, nsh, AF.Exp, [t_nsh], [t_nsh], scale=0.5)
                        ts("dve", nsh, nsh, -0.125 * 1.02, None, ALU.mult, None, [t_nsh], [t_nsh])
                        ts("dve", nsh, nsh, B8S(h), None, ALU.subtract, None, [t_nsh, tc_], [t_nsh])
                    units.append(fin)
                    return units

                for u_ in prologue_units(0):
                    u_()
                for h in range(8):
                    hp = h % 2
                    qT, kT = qT2[hp], kT2[hp]
                    t_q, t_k, t_va = t_q2[hp], t_k2[hp], t_va2[hp]
                    vav = vav2[hp]
                    nsh = nsh2[hp]; t_nsh = t_nsh2[hp]
                    bg = prologue_units(h + 1) if h + 1 < 8 else []
                    hb_ = HB[:, h * 256:(h + 1) * 256]
                    iters = [(Tq, c, j) for Tq in range(4) for c in range(2) for j in range(4 * Tq + 4)]
                    bg_every = max(1, len(iters) // (len(bg) + 1)) if bg else 0

                    def emit_qk(k):
                        Tq, c, j = iters[k]
                        rs = slice(c * 64, (c + 1) * 64)
                        qlo = max(j, 4 * Tq)
                        ncols = (4 * Tq + 4 - qlo) * 128
                        pss = 4 + (k % 2)
                        es = k % NET
                        near = j >= 4 * Tq - 1
                        mm(PB[pss][:, 0:ncols], t_PB[pss], kT[rs, j * 128:(j + 1) * 128], qT[rs, qlo * 128:(4 * Tq + 4) * 128],
                           True, not near, [t_q, t_k])
                        if j >= 4 * Tq:
                            nb = min(256, ncols)
                            mm(PB[pss][:, 0:nb], t_PB[pss], Jb, hb_[:, 0:nb], False, True, [t_const])
                        elif j == 4 * Tq - 1:
                            mm(PB[pss][:, 0:128], t_PB[pss], Jb, hb_[:, 128:256], False, True, [t_const])
                        act(ET[es][:, 0:ncols], PB[pss][:, 0:ncols], AF.Exp, [t_PB[pss], t_nsh], [t_ET[es]],
                            bias=nsh[:, c:c + 1], scale=0.125)

                    def emit_pv(k):
                        Tq, c, j = iters[k]
                        qlo = max(j, 4 * Tq)
                        ncols = (4 * Tq + 4 - qlo) * 128
                        col0 = (qlo - 4 * Tq) * 128
                        es = k % NET
                        sgi = segidx[(Tq, c)]
                        ob, db = 2 * (sgi % 2), 2 * (sgi % 2) + 1
                        last = (j == 4 * Tq + 3)
                        mm(PB[ob][:, col0:col0 + ncols], t_PB[ob], vav[:, j, 0:128], ET[es][:, 0:ncols], j == 0, last, [t_ET[es], t_va])
                        mm(PB[db][:, col0:col0 + ncols], t_PB[db], onesb, ET[es][:, 0:ncols], j == 0, last, [t_ET[es], t_const])
                        if last:
                            tC = tAB[c]
                            act(rcp, PB[db][:, :], AF.Ln, [t_PB[db], t_ep], [t_ep])
                            act(rcp, rcp, AF.Exp, [t_ep], [t_ep], scale=-1.0)
                            tt("dve", tC, PB[ob][:, :], rcp, ALU.mult, [t_PB[ob], t_ep], [t_ep])
                            if c == 1:
                                epilogue(Tq, db)

                    def epilogue(Tq, db):
                        for dd_ in list(deferred):
                            deferred.remove(dd_)
                            dd_[1]()
                        tA, tB = tAB
                        stt(tA, tB, NEGLAM, tA, ALU.mult, ALU.add, [t_ep, tc_], [t_ep])
                        tt("dve", sqb, tA, tA, ALU.mult, [t_ep], [t_ep])

                        def part2(h=h, Tq=Tq, db=db):
                            mm(PB[db][:, :], t_PB[db], onesb, sqb, True, True, [t_ep, t_const])
                            ts("dve", rcp, PB[db][:, :], 1.0 / 128.0, EPS, ALU.mult, ALU.add, [t_PB[db], t_ep], [t_ep])
                            act(rcp, rcp, AF.Ln, [t_ep], [t_ep])
                            act(rcp, rcp, AF.Exp, [t_ep], [t_ep], scale=-0.5)
                            stt(mview(8 + h, Tq * 512, 512), tA, GCOL, rcp, ALU.mult, ALU.mult, [t_ep, tc_], [mT[8 + h][Tq], t_ep])
                        deferred.append([3, part2])

                    for k in range(len(iters) + LA):
                        if k < len(iters):
                            emit_qk(k)
                        if k - LA >= 0:
                            emit_pv(k - LA)
                        for dd_ in list(deferred):
                            dd_[0] -= 1
                            if dd_[0] <= 0:
                                deferred.remove(dd_)
                                dd_[1]()
                        if bg and k % bg_every == bg_every - 1:
                            bg.pop(0)()
                    while bg:
                        bg.pop(0)()
                for dd_ in deferred:
                    dd_[1]()
                deferred = []
                S.barrier()

                if debug:
                    for e_ in range(DC):
                        dma("pool", DBG[:, e_ * SEQ:(e_ + 1) * SEQ], mview(e_, 0, SEQ), reads=[mT[e_][i] for i in range(4)])
                    S.barrier()
                    continue

                car = Carver()
                TT = 512
                y = car.take(DC * TT, F32)
                yv = lambda c: y[:, c * TT:(c + 1) * TT]
                t_y = [T("y%d" % c) for c in range(DC)]
                hb = car.take(DC * TT, BF16)
                hbv = lambda c: hb[:, c * TT:(c + 1) * TT]
                t_hb = [T("hb%d" % c) for c in range(DC)]
                NA = 15
                ab = car.take(16 * TT, BF16)
                abv = lambda j: ab[:, j * TT:(j + 1) * TT]
                t_ab = [T("a%d" % j) for j in range(NA)]
                t_ab16 = T("a16")
                xin0_ = car.take(D, F32)
                if car.off % 2:
                    car.off += 1
                off_x1 = car.off
                xin1_ = car.take(D, F32)
                xin = [xin0_, xin1_]; t_xin = [T("xin0"), T("xin1")]
                wppb = arena[:, off_x1:off_x1 + 2 * D]
                d_wpp = S.new_dma_slot() if s == 0 else d_wpp
                ost = [car.take(D, F32) for _ in range(2)]; t_ost = [T("ost0"), T("ost1")]
                tmpf = [car.take(TT, F32) for _ in range(3)]; t_tmp = [T("tmp%d" % i) for i in range(3)]
                rstd = car.take(TT, F32); t_rstd = T("rstd")
                mean_sb = tmpf[0]; nmr = tmpf[1]; vtmp = tmpf[2]
                pin = car.take(4 * PLE, F32); t_pin = T("pin")
                pbb = car.take(4 * PLE, BF16); t_pbb = T("pbb")
                pT = car.take(2 * TT, BF16); t_pT = T("pT")
                tmi = [0]
                bki = [0]

                def nbank():
                    bki[0] += 1
                    return bki[0] % 6

                def layer_norm(gk, bk, final):
                    y4 = lambda g4: y[:, 4 * g4 * TT:(4 * g4 + 4) * TT]
                    for g4 in range(4):
                        cs = list(range(4 * g4, 4 * g4 + 4))
                        cp("dve", hb[:, 4 * g4 * TT:(4 * g4 + 4) * TT], y4(g4), [t_y[c] for c in cs], [t_hb[c] for c in cs])
                        act(ab[:, 4 * g4 * TT:(4 * g4 + 4) * TT], y4(g4), AF.Square, [t_y[c] for c in cs], [t_ab[min(c, NA - 1)] for c in cs] + [t_ab16])
                    be = nbank()
                    for c in range(DC):
                        mm(PB[6][:, :], t_PB[6], meanm, hbv(c), c == 0, c == DC - 1, [t_const, t_hb[c]])
                    for c in range(DC):
                        mm(PB[be][:, :], t_PB[be], meanm, ab[:, c * TT:(c + 1) * TT], c == 0, c == DC - 1, [t_const, t_ab[min(c, NA - 1)], t_ab16])
                    t_st = t_rstd
                    cp("act", mean_sb, PB[6][:, :], [t_PB[6], t_tmp[0]], [t_st, t_tmp[0]])
                    tt("dve", vtmp, mean_sb, mean_sb, ALU.mult, [t_st, t_tmp[2]], [t_st, t_tmp[2]])
                    tt("dve", vtmp, PB[be][:, :], vtmp, ALU.subtract, [t_PB[be], t_st], [t_st, t_tmp[2]])
                    ts("dve", vtmp, vtmp, 0.0, EPS, ALU.max, ALU.add, [t_st], [t_st, t_tmp[2]])
                    act(rstd, vtmp, AF.Ln, [t_st], [t_st])
                    act(rstd, rstd, AF.Exp, [t_st], [t_st], scale=-0.5)
                    stt(nmr, mean_sb, -1.0, rstd, ALU.mult, ALU.mult, [t_st, t_tmp[1]], [t_st, t_tmp[1]])
                    rs3 = rstd.unsqueeze(1).broadcast_to([128, 4, TT])
                    nm3 = nmr.unsqueeze(1).broadcast_to([128, 4, TT])
                    for g4 in range(4):
                        cs = list(range(4 * g4, 4 * g4 + 4))
                        y43 = y4(g4).rearrange("p (a t) -> p a t", t=TT)
                        tt("dve", y43, y43, rs3, ALU.mult, [t_y[c] for c in cs] + [t_st], [t_y[c] for c in cs])
                        tt("dve", y43, y43, nm3, ALU.add, [t_y[c] for c in cs] + [t_st], [t_y[c] for c in cs])
                        for c in cs:
                            if final:
                                act(yv(c), yv(c), AF.Identity, [t_y[c], tc_], [t_y[c]], bias=LNP(bk, c), scale=LNP(gk, c))
                            else:
                                act(hbv(c), yv(c), AF.Identity, [t_y[c], tc_], [t_hb[c]], bias=LNP(bk, c), scale=LNP(gk, c))
                                act(yv(c), yv(c), AF.Identity, [t_y[c], tc_], [t_y[c]], bias=GA_(bk, c), scale=GA_(gk, c))

                xpre = {}

                def issue_xload(ttok_, b_):
                    if (ttok_, b_) in xpre:
                        return
                    xpre[(ttok_, b_)] = True
                    sl_ = b_ % 2
                    S.op("sp", lambda e, sl_=sl_, b_=b_, ttok_=ttok_, xin=xin: e.dma_start(out=xin[sl_], in_=X[ttok_ + b_ * 128:ttok_ + (b_ + 1) * 128, :]),
                         writes=[t_xin[sl_]], dma=True, dma_slot=d_xin[sl_])

                for t in range(SEQ // TT):
                    ttok = tok0 + t * TT
                    for b in range(4):
                        sl = b % 2
                        issue_xload(ttok, b)
                        for g in range(4):
                            bk = nbank()
                            for a in range(4):
                                c = 4 * g + a
                                S.op("pe", lambda e, bk=bk, a=a, c=c, sl=sl, xin=xin: e.transpose(PB[bk][:, a * 128:(a + 1) * 128],
                                                                                                  xin[sl][:, c * 128:(c + 1) * 128], identF[:]),
                                     reads=[t_xin[sl], t_const], writes=[t_PB[bk]])
                            dst = y[:, 4 * g * TT:(4 * g + 4) * TT].rearrange("p (a t) -> p a t", t=TT)[:, :, b * 128:(b + 1) * 128]
                            src = PB[bk][:, :].rearrange("p (a t) -> p a t", t=128)
                            ts("dve", dst, src, ALPHA, None, ALU.mult, None, [t_PB[bk]], [t_y[4 * g + a_] for a_ in range(4)])
                    for c in range(DC):
                        sl = load_w(Wo[:, :, c * 128:(c + 1) * 128], DC)
                        bk = nbank()
                        for e_ in range(DC):
                            mm(PB[bk][:, :], t_PB[bk], wk(sl, e_), mview(e_, t * TT, TT), e_ == 0, e_ == DC - 1, [t_wsl[sl], mT[e_][t]])
                        tt("dve", yv(c), yv(c), PB[bk][:, :], ALU.add, [t_y[c], t_PB[bk]], [t_y[c]])
                    layer_norm(0, 1, False)
                    for (j0, nj) in ((0, 15), (15, 15), (30, 14)):
                        for jj in range(nj):
                            j = j0 + jj
                            slg = load_w(Wg[:, :, j * 128:(j + 1) * 128], DC)
                            slu = load_w(Wu[:, :, j * 128:(j + 1) * 128], DC)
                            bg = nbank(); bu = nbank()
                            for d in range(DC):
                                mm(PB[bg][:, :], t_PB[bg], wk(slg, d), hbv(d), d == 0, d == DC - 1, [t_wsl[slg], t_hb[d]])
                            for d in range(DC):
                                mm(PB[bu][:, :], t_PB[bu], wk(slu, d), hbv(d), d == 0, d == DC - 1, [t_wsl[slu], t_hb[d]])
                            k_ = tmi[0] % 3; tmi[0] += 1
                            act(tmpf[k_], PB[bg][:, :], AF.Silu, [t_PB[bg]], [t_tmp[k_]])
                            tt("dve", abv(jj), tmpf[k_], PB[bu][:, :], ALU.mult, [t_tmp[k_], t_PB[bu]], [t_ab[jj]])
                        for c in range(DC):
                            sl = load_w(Wd[:, j0:j0 + nj, c * 128:(c + 1) * 128], nj)
                            bk = nbank()
                            for jj in range(nj):
                                mm(PB[bk][:, :], t_PB[bk], wk(sl, jj), abv(jj), jj == 0, jj == nj - 1, [t_wsl[sl], t_ab[jj]])
                            tt("dve", yv(c), yv(c), PB[bk][:, :], ALU.add, [t_y[c], t_PB[bk]], [t_y[c]])
                    layer_norm(2, 3, False)
                    for b in range(4):
                        S.op("sp", lambda e, b=b, ttok=ttok, pin=pin: e.dma_start(out=pin[:, b * PLE:(b + 1) * PLE], in_=P_[ttok + b * 128:ttok + (b + 1) * 128, :]),
                             writes=[t_pin], dma=True, dma_slot=d_p)
                    cp("dve", pbb, pin, [t_pin], [t_pbb])
                    for b in range(4):
                        for k2 in range(2):
                            S.op("pe", lambda e, b=b, k2=k2: e.transpose(PBF[0][:, k2 * 128:(k2 + 1) * 128],
                                                                         pbb[:, b * PLE + k2 * 128:b * PLE + (k2 + 1) * 128], identB),
                                 reads=[t_pbb, t_const], writes=[t_PBF[0]])
                        dst = pT.rearrange("p (k t) -> p k t", t=TT)[:, :, b * 128:(b + 1) * 128]
                        evac(dst, PBF[0][:, 0:256].rearrange("p (k t) -> p k t", t=128), [t_PBF[0]], [t_pT])
                    for c in range(DC):
                        if c == 0:
                            S.op("pool", lambda e, wppb=wppb: e.dma_start(out=wppb.rearrange("p (k c) -> p k c", c=D), in_=Wpp[:, :, :]),
                                 writes=[t_xin[1]], dma=True, dma_slot=d_wpp)
                        slg = load_w(Wpg[:, :, c * 128:(c + 1) * 128], DC)
                        bg = nbank(); be = nbank()
                        for d in range(DC):
                            mm(PB[bg][:, :], t_PB[bg], wk(slg, d), hbv(d), d == 0, d == DC - 1, [t_wsl[slg], t_hb[d]])
                        for k2 in range(2):
                            wpp_ = wppb[:, k2 * D + c * 128:k2 * D + (c + 1) * 128]
                            mm(PB[be][:, :], t_PB[be], wpp_, pT[:, k2 * TT:(k2 + 1) * TT], k2 == 0, k2 == 1, [t_xin[1], t_pT])
                        k_ = tmi[0] % 3; tmi[0] += 1
                        act(tmpf[k_], PB[bg][:, :], AF.Sigmoid, [t_PB[bg], tc_], [t_tmp[k_]], bias=LNP(6, c))
                        tt("dve", tmpf[k_], tmpf[k_], PB[be][:, :], ALU.mult, [t_tmp[k_], t_PB[be]], [t_tmp[k_]])
                        tt("dve", yv(c), yv(c), tmpf[k_], ALU.add, [t_y[c], t_tmp[k_]], [t_y[c]])
                    if t + 1 < SEQ // TT:
                        issue_xload(ttok + TT, 0)
                        issue_xload(ttok + TT, 1)
                    layer_norm(4, 5, True)
                    for b in range(4):
                        sl = b % 2
                        for g in range(4):
                            bk = nbank()
                            for a in range(4):
                                c = 4 * g + a
                                S.op("pe", lambda e, bk=bk, a=a, src_=yv(c)[:, b * 128:(b + 1) * 128]: e.transpose(PB[bk][:, a * 128:(a + 1) * 128], src_, identF[:]),
                                     reads=[t_y[c], t_const], writes=[t_PB[bk]])
                            evac(ost[sl][:, g * 512:(g + 1) * 512], PB[bk][:, :], [t_PB[bk]], [t_ost[sl]])
                        S.op("sp", lambda e, sl=sl, b=b, ttok=ttok, ost=ost: e.dma_start(out=OUT[ttok + b * 128:ttok + (b + 1) * 128, :], in_=ost[sl]),
                             reads=[t_ost[sl]], dma=True, dma_slot=d_out[sl])
                S.barrier()
        except _Stop:
            if debug:
                S.barrier()
                for e_ in range(DC):
                    S.op("pool", lambda e, e_=e_: e.dma_start(out=DBG[:, e_ * SEQ:(e_ + 1) * SEQ], in_=mergedT[:, e_ * SEQ:(e_ + 1) * SEQ]),
                         reads=[mT[e_][i] for i in range(4)], dma=True, dma_slot=S.new_dma_slot())
        S.emit()
    return nc


def _host_inputs(inp, ncores=N_CORES, nseq=SPC):
    f = lambda a: np.ascontiguousarray(np.asarray(a, dtype=np.float32))
    cm, oh, mrow = _consts()
    chunkT = lambda v, nch: f(np.asarray(v).reshape(nch, 128).T)
    pvec = np.zeros((128, 176), np.float32)
    cw = np.asarray(inp["conv_w"])[0]
    pvec[:, 0:32] = cw.reshape(4, 8, 128).transpose(2, 1, 0).reshape(128, 32)
    pvec[:, 32:40] = chunkT(inp["conv_b"][0], 8)
    pvec[:, 40:48] = chunkT(inp["lru_ba"][0], 8)
    pvec[:, 48:56] = chunkT(inp["lru_bx"][0], 8)
    pvec[:, 56:64] = chunkT(inp["lru_lambda"][0], 8)
    for k, nm in enumerate(("ln1_g", "ln1_b", "ln2_g", "ln2_b", "ln3_g", "ln3_b", "b_ple_gate")):
        pvec[:, 64 + 16 * k:80 + 16 * k] = chunkT(inp[nm][0], 16)
    brow = np.zeros((1, 640), np.float32)
    brow[0, 0:64] = inp["diff_lq1"][0]
    brow[0, 64:128] = inp["diff_lk1"][0]
    brow[0, 128:192] = inp["diff_lq2"][0]
    brow[0, 192:256] = inp["diff_lk2"][0]
    brow[0, 256:384] = inp["diff_subln_g"][0]
    brow[0, 384:640] = np.asarray(inp["rel_bias"]).reshape(-1)
    lw = np.stack([np.asarray(inp["lru_wa"])[0], np.asarray(inp["lru_wx"])[0]], 0)
    lruw = f(lw.transpose(2, 0, 1, 3).reshape(128, 2 * 8 * 128))
    shared = {
        "w_in": f(inp["w_in"][0]), "w_out": f(inp["w_out"][0]), "w_gate": f(inp["w_ffn_gate"][0]),
        "w_up": f(inp["w_ffn_up"][0]), "w_down": f(inp["w_ffn_down"][0]), "w_pgate": f(inp["w_ple_gate"][0]),
        "w_pproj": f(inp["w_ple_proj"][0]), "lruw": lruw, "pvec": pvec, "brow": brow,
        "relb": f(inp["rel_bias"]), "cmat": cm, "oneh": oh, "mrow": mrow,
    }
    x = np.asarray(inp["x"], dtype=np.float32)
    p = np.asarray(inp["p"], dtype=np.float32)[0]
    maps = []
    for c in range(ncores):
        m = dict(shared)
        m["x"] = f(x[c * nseq:(c + 1) * nseq].reshape(nseq * SEQ, D))
        m["p"] = f(p[c * nseq:(c + 1) * nseq].reshape(nseq * SEQ, PLE))
        maps.append(m)
    return maps


def kernel(**inputs):
    nc = build()
    maps = _host_inputs(inputs)
    res = run_bass_kernel_spmd(nc, maps, core_ids=list(range(N_CORES)))
    outs = [np.asarray(r["out"]).reshape(SPC, SEQ, D) for r in res.results]
    return np.concatenate(outs, axis=0).astype(np.float32)
```

```python
import contextlib
import math
import numpy as np
import concourse.bass as bass
import concourse.mybir as mybir
from concourse.bass_utils import run_bass_kernel_spmd

F32 = mybir.dt.float32
BF16 = mybir.dt.bfloat16
AF = mybir.ActivationFunctionType
ALU = mybir.AluOpType
AX = mybir.AxisListType

N_CORES = 8
D = 2048
SEQ = 2048
BATCH = 16
SPC = BATCH // N_CORES
DC = D // 128
IN_W = 5120
FFN = 5632
FC = FFN // 128
PLE = 256
NTB = SEQ // 128
ALPHA = 2.0 ** 0.25
LAM_INIT = 0.2
EPS = 1e-5
LRU_C = 8.0
NEG = -30000.0

ENGS = ("pe", "act", "dve", "pool", "sp")


class T:
    __slots__ = ("name", "w", "r", "rd")

    def __init__(self, name=""):
        self.name = name
        self.w = None
        self.r = {}
        self.rd = []


class Op:
    __slots__ = ("eng", "fn", "deps", "idx", "signal", "sigval", "dma", "sem", "semval", "ndma")

    def __init__(self, eng, fn, dma=False, ndma=1):
        self.eng = eng
        self.fn = fn
        self.deps = []
        self.signal = False
        self.sigval = 0
        self.dma = dma
        self.sem = None
        self.semval = 0
        self.ndma = ndma
        self.idx = 0


class Sched:
    def __init__(self, nc):
        self.nc = nc
        self.ops = {e: [] for e in ENGS}
        self.n_slots = 0
        self.all_ops = []
        self.pending = {}
        self.last_dma = {}

    def op(self, eng, fn, reads=(), writes=(), dma=False, dma_slot=None, ndma=1):
        o = Op(eng, fn, dma=dma, ndma=ndma)
        o.idx = len(self.ops[eng])
        best = {}
        dd = []

        def add(d):
            if d is None or d is o:
                return
            if d.dma:
                if d not in dd:
                    dd.append(d)
            else:
                b = best.get(d.eng)
                if b is None or d.idx > b.idx:
                    best[d.eng] = d

        for t in reads:
            add(t.w)
        for t in writes:
            add(t.w)
            for x in t.r.values():
                add(x)
            for x in t.rd:
                add(x)
        for x in self.pending.pop(eng, ()):
            add(x)
        o.deps = list(best.values()) + dd
        for t in reads:
            if dma:
                t.rd.append(o)
            else:
                t.r[eng] = o
        for t in writes:
            t.w = o
            t.r = {}
            t.rd = []
        if dma:
            o.sem = dma_slot
            self.last_dma[dma_slot] = o
        self.ops[eng].append(o)
        self.all_ops.append(o)
        return o

    def new_dma_slot(self):
        self.n_slots += 1
        return self.n_slots - 1

    def barrier(self):
        lasts = []
        for e in ENGS:
            for o in reversed(self.ops[e]):
                if not o.dma:
                    lasts.append(o)
                    break
        lasts.extend(self.last_dma.values())
        for e in ENGS:
            self.pending[e] = list(lasts) + list(self.pending.get(e, ()))

    def emit(self):
        nc = self.nc
        for o in self.all_ops:
            for d in o.deps:
                if (not d.dma) and d.eng == o.eng and o.eng == "pe":
                    continue
                d.signal = True
        for e in ENGS:
            c = 0
            for o in self.ops[e]:
                if o.dma:
                    continue
                if o.signal:
                    c += 1
                    o.sigval = c
        cnt = [0] * self.n_slots
        for o in self.all_ops:
            if o.dma:
                cnt[o.sem] += 16 * o.ndma
                o.semval = cnt[o.sem]
        with contextlib.ExitStack() as st:
            eng_sem = {e: st.enter_context(nc.semaphore("prog_" + e)) for e in ENGS}
            dsem = [st.enter_context(nc.semaphore("dma_%d" % i)) for i in range(self.n_slots)]
            block = st.enter_context(nc.Block())

            def run_engine(ename, eng):
                waited = {}
                for o in self.ops[ename]:
                    for d in o.deps:
                        if d.dma:
                            key = ("d", d.sem)
                            s, v = dsem[d.sem], d.semval
                        else:
                            if d.eng == ename and ename == "pe":
                                continue
                            key = ("e", d.eng)
                            s, v = eng_sem[d.eng], d.sigval
                        if waited.get(key, 0) >= v:
                            continue
                        waited[key] = v
                        eng.wait_ge(s, v)
                    ins = o.fn(eng)
                    if o.dma:
                        if not isinstance(ins, (list, tuple)):
                            ins = [ins]
                        assert len(ins) == o.ndma
                        for i_ in ins:
                            i_.then_inc(dsem[o.sem], 16)
                    elif o.signal:
                        ins.then_inc(eng_sem[ename], 1)
                for o in self.ops[ename]:
                    if o.dma:
                        key = ("d", o.sem)
                        if waited.get(key, 0) < o.semval:
                            waited[key] = o.semval
                            eng.wait_ge(dsem[o.sem], o.semval)

            @block.tensor
            def _(eng):
                run_engine("pe", eng)

            @block.scalar
            def _(eng):
                run_engine("act", eng)

            @block.vector
            def _(eng):
                run_engine("dve", eng)

            @block.gpsimd
            def _(eng):
                run_engine("pool", eng)

            @block.sync
            def _(eng):
                run_engine("sp", eng)


def _bucket(n):
    max_exact = 16
    nf = np.maximum(n, 1).astype(np.float32)
    large = max_exact + (np.log(nf / np.float32(max_exact)) / np.float32(math.log(128 / max_exact))
                         * np.float32(32 - max_exact)).astype(np.int32)
    large = np.minimum(large, 31)
    return np.where(n < max_exact, n, large)


def _consts():
    cm = np.zeros((128, 6 * 128), np.float32)
    cm[:, 0:128] = np.eye(128)
    cm[:, 128:256] = np.eye(128)[::-1]
    cm[0:64, 256:384] = 1.0
    cm[64:128, 384:512] = 1.0
    cm[:, 512:640] = 1.0 / D
    cm[:, 640:768] = 1.0
    oh = np.zeros((32, 384), np.float32)
    mrow = np.zeros((8, 384), np.float32)
    for m in range(384):
        n = m - 127
        if n < 0:
            mrow[:, m] = NEG
        else:
            b = int(_bucket(np.array([n]))[0])
            oh[b, m] = 1.0
    return cm, oh, mrow


class _Stop(Exception):
    pass


def build(nseq=SPC, debug=False, stop=None):
    nc = bass.Bass("TRN2", target_bir_lowering=False)
    NTOK = nseq * SEQ
    dram = lambda name, shape, kind="ExternalInput": nc.dram_tensor(name, shape, F32, kind=kind).ap()
    X = dram("x", [NTOK, D])
    P_ = dram("p", [NTOK, PLE])
    W_IN = dram("w_in", [D, IN_W])
    W_OUT = dram("w_out", [D, D])
    W_G = dram("w_gate", [D, FFN])
    W_U = dram("w_up", [D, FFN])
    W_D = dram("w_down", [FFN, D])
    W_PG = dram("w_pgate", [D, D])
    W_PP = dram("w_pproj", [PLE, D])
    LRUW = dram("lruw", [128, 2 * 8 * 128])
    PVEC = dram("pvec", [128, 176])
    BROW = dram("brow", [1, 640])
    RELB = dram("relb", [32, 8])
    CMAT = dram("cmat", [128, 768])
    ONEH = dram("oneh", [32, 384])
    MROW = dram("mrow", [8, 384])
    OUT = dram("out", [NTOK, D], kind="ExternalOutput")
    SCR = dram("scr", [8, 384], kind="Internal")
    dram16 = lambda name, shape: nc.dram_tensor(name, shape, BF16, kind="Internal").ap()
    W_OUT16 = dram16("w_out16", [D, D])
    W_G16 = dram16("w_gate16", [D, FFN])
    W_U16 = dram16("w_up16", [D, FFN])
    W_D16 = dram16("w_down16", [FFN, D])
    W_PG16 = dram16("w_pgate16", [D, D])
    DBG = dram("dbg", [128, 16 * SEQ], kind="ExternalOutput") if debug else None

    S = Sched(nc)
    with contextlib.ExitStack() as st:
        sbt = lambda name, shape, dt=F32: st.enter_context(nc.sbuf_tensor(name, shape, dt))
        identF = sbt("identF", [128, 128], F32)
        cb = sbt("cb", [128, 768], BF16)
        identB, Jb, sel0, sel1, meanm, onesb = (cb[:, i * 128:(i + 1) * 128] for i in range(6))
        pv = sbt("pv", [128, 176], F32)
        der = sbt("der", [128, 96], F32)
        lruw = sbt("lruw_sb", [128, 2 * 8 * 128], BF16)
        HB = sbt("HB", [128, 8 * 256], BF16)
        gsub = sbt("gsub", [128, 128], F32)
        sm = sbt("sm", [128, 64], F32)
        mergedT = sbt("mergedT", [128, DC * SEQ], BF16)
        NSLOT = 4
        wsl = [sbt("wslot%d" % i, [128, 16 * 128], BF16) for i in range(NSLOT)]
        import os
        ARENA = int(os.environ.get('ARENA', '59400'))
        arena = sbt("arena", [128, ARENA], BF16)

        t_const = T("const")
        t_wsl = [T("wsl%d" % i) for i in range(NSLOT)]
        d_wsl = [S.new_dma_slot() for _ in range(NSLOT)]
        mT = [[T("mT%d_%d" % (e, tt)) for tt in range(4)] for e in range(DC)]

        PB = [st.enter_context(nc.psum_tensor("pb%d" % i, [128, 512], F32)) for i in range(7)]
        t_PB = [T("pb%d" % i) for i in range(7)]
        pbf_all = st.enter_context(nc.psum_tensor("pbf0", [128, 1024], BF16))
        PBF = [pbf_all[:, 0:512]]
        t_PBF = [T("pbf0")]

        class Carver:
            def __init__(self):
                self.off = 0

            def take(self, nelem, dt):
                if dt == F32:
                    if self.off % 2:
                        self.off += 1
                    a = arena[:, self.off:self.off + 2 * nelem].bitcast(F32)
                    self.off += 2 * nelem
                else:
                    a = arena[:, self.off:self.off + nelem]
                    self.off += nelem
                assert self.off <= ARENA, self.off
                return a

        def dma(q, out, in_, reads=(), writes=(), slot=None, allow=False):
            if slot is None:
                slot = S.new_dma_slot()
            if allow:
                f = lambda e: e.dma_start(out=out, in_=in_, allow_slow_non_contiguous=True)
            else:
                f = lambda e: e.dma_start(out=out, in_=in_)
            return S.op(q, f, reads=reads, writes=writes, dma=True, dma_slot=slot)

        wi = [0]

        conv_hook = [None]

        def load_w(src_ap, nk, extra_reads=()):
            s = wi[0] % NSLOT
            wi[0] += 1
            dst = wsl[s][:, 0:nk * 128].rearrange("p (k c) -> p k c", c=128)
            S.op("pool", lambda e: e.dma_start(out=dst, in_=src_ap), reads=list(extra_reads), writes=[t_wsl[s]], dma=True, dma_slot=d_wsl[s])
            if conv_hook[0] is not None and not extra_reads:
                conv_hook[0](5)
            return s

        def wk(s, k):
            return wsl[s][:, k * 128:(k + 1) * 128]

        def mm(ps_ap, t_ps, lhsT, rhs, start, stop, reads):
            return S.op("pe", lambda e: e.matmul(ps_ap, lhsT=lhsT, rhs=rhs, start=start, stop=stop),
                        reads=reads, writes=[t_ps])

        def act(out, in_, func, reads, writes, bias=None, scale=None, accum_out=None):
            kw = {}
            if bias is not None:
                kw["bias"] = bias
            if scale is not None:
                kw["scale"] = scale
            if accum_out is not None:
                kw["accum_out"] = accum_out
            return S.op("act", lambda e: e.activation(out=out, in_=in_, func=func, **kw), reads=reads, writes=writes)

        def ts(eng, out, in0, s1, s2, op0, op1, reads, writes):
            if s2 is None:
                return S.op(eng, lambda e: e.tensor_scalar(out=out, in0=in0, scalar1=s1, scalar2=None, op0=op0),
                            reads=reads, writes=writes)
            return S.op(eng, lambda e: e.tensor_scalar(out=out, in0=in0, scalar1=s1, scalar2=s2, op0=op0, op1=op1),
                        reads=reads, writes=writes)

        def tt(eng, out, in0, in1, op, reads, writes):
            return S.op(eng, lambda e: e.tensor_tensor(out=out, in0=in0, in1=in1, op=op), reads=reads, writes=writes)

        def stt(out, in0, scalar, in1, op0, op1, reads, writes):
            return S.op("dve", lambda e: e.scalar_tensor_tensor(out=out, in0=in0, scalar=scalar, in1=in1, op0=op0, op1=op1),
                        reads=reads, writes=writes)

        def cp(eng, out, in_, reads, writes):
            if eng == "act":
                return act(out, in_, AF.Identity, reads, writes)
            return S.op(eng, lambda e: e.tensor_copy(out=out, in_=in_), reads=reads, writes=writes)

        evi = [0]

        def evac(out, in_, reads, writes):
            evi[0] += 1
            import os
            if os.environ.get("EVAC_DVE"):
                return cp("dve", out, in_, reads, writes)
            return cp("act" if evi[0] % 2 else "dve", out, in_, reads, writes)

        try:
            car0 = Carver()
            bc = car0.take(640, F32)
            dma("sp", identF[:], CMAT[:, 0:128], writes=[t_const])
            dma("pool", cb[:], CMAT[:, 0:768], writes=[t_const])
            dma("sp", pv[:], PVEC[:, :], writes=[t_const])
            dma("sp", bc, BROW.partition_broadcast(128), writes=[t_const])
            dma("pool", lruw[:], LRUW[:, :], writes=[t_const])
            if stop == 'S1':
                raise _Stop()
            CW = lambda n, j: pv[:, n * 4 + j:n * 4 + j + 1]
            CBc = lambda n: pv[:, 32 + n:33 + n]
            BA = lambda n: pv[:, 40 + n:41 + n]
            BX = lambda n: pv[:, 48 + n:49 + n]
            LNP = lambda k, c: pv[:, 64 + k * 16 + c:64 + k * 16 + c + 1]
            LSC = lambda n: der[:, n:n + 1]
            LSC2 = lambda n: der[:, 8 + n:9 + n]
            GA_ = lambda k, c: der[:, 16 + k * 16 + c:16 + k * 16 + c + 1]
            NEGLAM = der[:, 80:81]
            EPSC = der[:, 81:82]
            ONEC = der[:, 82:83]
            B8S = lambda h: der[:, 84 + h:85 + h]
            GCOL = der[:, 83:84]
            tc_ = t_const
            S.op("dve", lambda e: e.memset(der[:, 81:82], EPS), writes=[tc_])
            S.op("dve", lambda e: e.memset(der[:, 82:83], 1.0), writes=[tc_])
            ts("dve", der[:, 16:80], pv[:, 64:128], ALPHA, None, ALU.mult, None, [tc_], [tc_])
            L = pv[:, 56:64]
            m_ = sm[:, 0:8]; z_ = sm[:, 8:16]; w_ = sm[:, 16:24]; w2 = sm[:, 24:32]; pl = sm[:, 32:40]
            ts("dve", m_, L, -1.0, None, ALU.mult, None, [tc_], [tc_])
            tt("dve", z_, m_, L, ALU.max, [tc_], [tc_])
            ts("dve", m_, m_, 0.0, None, ALU.max, None, [tc_], [tc_])
            act(z_, z_, AF.Exp, [tc_], [tc_], scale=-1.0)
            ts("dve", w_, z_, 2.0, None, ALU.add, None, [tc_], [tc_])
            S.op("dve", lambda e: e.reciprocal(out=w_, in_=w_), reads=[tc_], writes=[tc_])
            tt("dve", w_, w_, z_, ALU.mult, [tc_], [tc_])
            tt("dve", w2, w_, w_, ALU.mult, [tc_], [tc_])
            ts("dve", pl, w2, 1.0 / 9.0, 1.0 / 7.0, ALU.mult, ALU.add, [tc_], [tc_])
            for cf in (1.0 / 5.0, 1.0 / 3.0, 1.0):
                tt("dve", pl, pl, w2, ALU.mult, [tc_], [tc_])
                ts("dve", pl, pl, cf, None, ALU.add, None, [tc_], [tc_])
            tt("dve", pl, pl, w_, ALU.mult, [tc_], [tc_])
            ts("dve", pl, pl, 2.0, None, ALU.mult, None, [tc_], [tc_])
            tt("dve", pl, pl, m_, ALU.add, [tc_], [tc_])
            ts("dve", der[:, 0:8], pl, -LRU_C, None, ALU.mult, None, [tc_], [tc_])
            ts("dve", der[:, 8:16], pl, -2.0 * LRU_C, None, ALU.mult, None, [tc_], [tc_])
            if stop == 'S2':
                raise _Stop()
            pr = sm[:, 40:42]
            tmpl = sm[:, 42:44]
            for k in range(2):
                S.op("dve", lambda e, k=k: e.tensor_tensor(out=gsub[:, 0:64], in0=bc[:, k * 128:k * 128 + 64],
                                                           in1=bc[:, k * 128 + 64:k * 128 + 128], op=ALU.mult),
                     reads=[tc_], writes=[tc_])
                S.op("dve", lambda e, k=k: e.reduce_sum(out=pr[:, k:k + 1], in_=gsub[:, 0:64], axis=AX.X), reads=[tc_], writes=[tc_])
            act(pr, pr, AF.Exp, [tc_], [tc_])
            tt("dve", tmpl[:, 0:1], pr[:, 1:2], pr[:, 0:1], ALU.subtract, [tc_], [tc_])
            ts("dve", der[:, 80:81], tmpl[:, 0:1], -LAM_INIT, None, ALU.add, None, [tc_], [tc_])
            dma("sp", der[:, 83:84], BROW[0:1, 256:384].rearrange("a d -> d a"), writes=[tc_], allow=True)
            ts("dve", der[:, 83:84], der[:, 83:84], 1.0 - LAM_INIT, None, ALU.mult, None, [tc_], [tc_])
            ts("dve", gsub[:], bc[:, 256:384], 1.0 - LAM_INIT, None, ALU.mult, None, [tc_], [tc_])
            bm = sm[:, 44:52]
            S.op("dve", lambda e: e.reduce_max(out=bm, in_=bc[:, 384:640].rearrange("p (b h) -> p h b", h=8), axis=AX.X),
                 reads=[tc_], writes=[tc_])
            tt("dve", bm, bm, bc[:, 384 + 248:384 + 256], ALU.subtract, [tc_], [tc_])
            ts("dve", der[:, 84:92], bm, 0.0, None, ALU.max, None, [tc_], [tc_])
            if stop == 'S3':
                raise _Stop()
            relb_sb = car0.take(8, F32)[0:32, :]
            oneh_sb = car0.take(384, F32)[0:32, :]
            mrow_sb = car0.take(384, F32)[0:8, :]
            c8 = car0.take(1, F32)[0:8, :]
            vec8 = car0.take(384, F32)[0:8, :]
            hst = car0.take(256, F32)
            t_set = T("setup")
            dma("sp", relb_sb, RELB[:, :], writes=[t_set])
            dma("sp", oneh_sb, ONEH[:, :], writes=[t_set])
            dma("sp", mrow_sb, MROW[:, :], writes=[t_set])
            dma("sp", c8, RELB[31:32, :].rearrange("a h -> h a"), writes=[t_set], allow=True)
            mm(PB[0][0:8, 0:384], t_PB[0], relb_sb, oneh_sb, True, True, [t_set])
            ts("dve", vec8, PB[0][0:8, 0:384], c8, 8.0, ALU.subtract, ALU.mult, [t_PB[0], t_set], [t_set])
            tt("dve", vec8, vec8, mrow_sb, ALU.add, [t_set], [t_set])
            if stop == 'S4':
                raise _Stop()
            t_scr = T("scr")
            dma("sp", SCR[:, :], vec8, reads=[t_set], writes=[t_scr])
            for h in range(8):
                hk = bass.AP(tensor=SCR.tensor, offset=h * 384, ap=[[1, 128], [1, 256]])
                dma("sp", hst, hk, reads=[t_scr], writes=[t_set])
                cp("dve", HB[:, h * 256:(h + 1) * 256], hst, [t_set], [tc_])
            S.barrier()
            if stop == 'setup':
                raise _Stop()

            conv_jobs = []
            conv_T = {}
            d_conv = {}
            for nm, src, dst, nrow, ncol in (("o", W_OUT, W_OUT16, D, D), ("g", W_G, W_G16, D, FFN), ("u", W_U, W_U16, D, FFN),
                                              ("d", W_D, W_D16, FFN, D), ("pg", W_PG, W_PG16, D, D)):
                conv_T[nm] = []
                d_conv[nm] = S.new_dma_slot()
                for r0 in range(0, nrow, 128):
                    for c0 in range(0, ncol, 2048):
                        cw_ = min(2048, ncol - c0)
                        conv_jobs.append((nm, src[r0:r0 + 128, c0:c0 + cw_], dst[r0:r0 + 128, c0:c0 + cw_]))

            def conv_some(n):
                for _ in range(n):
                    if not conv_jobs:
                        return
                    nm, src_, dst_ = conv_jobs.pop(0)
                    t_ = T("cv")
                    conv_T[nm].append(t_)
                    S.op("pool", lambda e, src_=src_, dst_=dst_: e.dma_start(out=dst_, in_=src_), writes=[t_], dma=True, dma_slot=d_conv[nm])

            conv_hook[0] = conv_some

            Wv = W_IN.rearrange("(c p) e -> p c e", p=128)
            Wo = W_OUT16.rearrange("(c p) e -> p c e", p=128)
            Wg = W_G16.rearrange("(c p) e -> p c e", p=128)
            Wu = W_U16.rearrange("(c p) e -> p c e", p=128)
            Wd = W_D16.rearrange("(c p) e -> p c e", p=128)
            Wpg = W_PG16.rearrange("(c p) e -> p c e", p=128)
            Wpp = W_PP.rearrange("(c p) e -> p c e", p=128)
            mview = lambda e, c0, n: mergedT[:, e * SEQ + c0:e * SEQ + c0 + n]
            d_xin = [S.new_dma_slot() for _ in range(2)]
            d_xbf = [S.new_dma_slot() for _ in range(4)]
            d_out = [S.new_dma_slot() for _ in range(2)]
            d_p = S.new_dma_slot()

            for s in range(nseq):
                tok0 = s * SEQ
                car = Carver()
                xT = car.take(DC * SEQ, BF16)
                xTv = lambda d, c0, n: xT[:, d * SEQ + c0:d * SEQ + c0 + n]
                t_xT = [T("xT%d" % i) for i in range(4)]
                mark = car.off
                NXB = 4
                xbf = [car.take(D, BF16) for _ in range(NXB)]
                t_xbf = [T("xbf%d" % i) for i in range(NXB)]
                abanks = [(PB[k_][:, 0:256].bitcast(BF16), t_PB[k_]) for k_ in range(7)] + [(PBF[0], t_PBF[0])]
                gi_ = 0
                for blk in range(NTB):
                    sl = blk % NXB
                    S.op("pool", lambda e, sl=sl, blk=blk, tok0=tok0, xbf=xbf: [e.dma_start(out=xbf[sl][:, hh * 1024:(hh + 1) * 1024],
                                                                        in_=X[tok0 + blk * 128:tok0 + (blk + 1) * 128, hh * 1024:(hh + 1) * 1024])
                                                            for hh in range(2)],
                         writes=[t_xbf[sl]], dma=True, dma_slot=d_xbf[sl], ndma=2)
                    for g in range(4):
                        bank_ap, t_bank = abanks[gi_ % len(abanks)]
                        gi_ += 1
                        for a in range(4):
                            c = 4 * g + a
                            S.op("pe", lambda e, bank_ap=bank_ap, a=a, c=c, sl=sl, xbf=xbf: e.transpose(bank_ap[:, a * 128:(a + 1) * 128],
                                                                                                     xbf[sl][:, c * 128:(c + 1) * 128], identB),
                                 reads=[t_xbf[sl], t_const], writes=[t_bank])
                        dst = xT[:, 4 * g * SEQ:(4 * g + 4) * SEQ].rearrange("p (a t) -> p a t", t=SEQ)[:, :, blk * 128:(blk + 1) * 128]
                        src = bank_ap.rearrange("p (a t) -> p a t", t=128)
                        evac(dst, src, [t_bank], [t_xT[blk // 4]])
                S.barrier()
                car.off = mark
                if stop == 'A0':
                    raise _Stop()

                def inproj_mm(col0):
                    sl = load_w(Wv[:, :, col0:col0 + 128], DC)
                    for d in range(DC):
                        for t4 in range(4):
                            mm(PB[t4][:, :], t_PB[t4], wk(sl, d), xTv(d, t4 * 512, 512), d == 0, d == DC - 1,
                               [t_wsl[sl], t_xT[t4]])

                def inproj_evac(dst_fn, dst_tiles):
                    for t4 in range(4):
                        evac(dst_fn(t4), PB[t4][:, :], [t_PB[t4]], dst_tiles)

                xrp2 = [car.take(SEQ + 4, F32) for _ in range(2)]; t_xrp2 = [T("xrp0"), T("xrp1")]
                ygb2 = [car.take(SEQ, F32) for _ in range(2)]; t_ygb2 = [T("ygb0"), T("ygb1")]
                QW = 512
                LS = []
                for si in range(2):
                    LS.append(dict(conv=car.take(QW, F32), ga=car.take(QW, F32), gx=car.take(QW, F32), A=car.take(QW, F32),
                                   convb=car.take(QW, BF16), t_conv=T("conv%d" % si), t_ga=T("ga%d" % si), t_gx=T("gx%d" % si),
                                   t_A=T("A%d" % si), t_convb=T("convb%d" % si)))
                carry4 = sm[:, 24:28]; t_carry = [T("carry%d" % i) for i in range(4)]
                for b_ in range(2):
                    S.op("dve", lambda e, b_=b_: e.memset(xrp2[b_][:, 0:4], 0.0), writes=[t_xrp2[b_]])
                gbank = [0]

                def lru_math(n, q):
                    L_ = LS[q % 2]
                    conv_h, ga_h, gx_h, A_h, convb = L_["conv"], L_["ga"], L_["gx"], L_["A"], L_["convb"]
                    t_conv, t_ga, t_gx, t_A, t_convb = L_["t_conv"], L_["t_ga"], L_["t_gx"], L_["t_A"], L_["t_convb"]
                    xrp = xrp2[n % 2]; t_xrp = t_xrp2[n % 2]
                    ygb = ygb2[n % 2]; t_ygb = t_ygb2[n % 2]
                    c0 = q * QW
                    ts("dve", conv_h, xrp[:, c0 + 3:c0 + 3 + QW], CW(n, 3), CBc(n), ALU.mult, ALU.add, [t_xrp, tc_], [t_conv]); yield
                    for j in (2, 1, 0):
                        stt(conv_h, xrp[:, c0 + j:c0 + j + QW], CW(n, j), conv_h, ALU.mult, ALU.add, [t_xrp, t_conv, tc_], [t_conv]); yield
                    cp("act", convb, conv_h, [t_conv], [t_convb]); yield
                    for g, (dstb, t_d, bcol) in enumerate(((ga_h, t_ga, BA(n)), (gx_h, t_gx, BX(n)))):
                        pbk = 4 + (gbank[0] % 3); gbank[0] += 1
                        wg_ = lruw[:, (g * 8 + n) * 128:(g * 8 + n + 1) * 128]
                        mm(PB[pbk][:, :], t_PB[pbk], wg_, convb, True, True, [tc_, t_convb])
                        act(dstb, PB[pbk][:, :], AF.Sigmoid, [t_PB[pbk], tc_], [t_d], bias=bcol); yield
                    act(A_h, ga_h, AF.Exp, [t_ga, tc_], [t_A], scale=LSC(n)); yield
                    act(ga_h, ga_h, AF.Exp, [t_ga, tc_], [t_ga], scale=LSC2(n)); yield
                    act(ga_h, ga_h, AF.Ln, [t_ga, tc_], [t_ga], scale=-(1.0 - 1e-6), bias=ONEC); yield
                    act(ga_h, ga_h, AF.Exp, [t_ga], [t_ga], scale=0.5); yield
                    tt("dve", gx_h, gx_h, ga_h, ALU.mult, [t_gx, t_ga], [t_gx]); yield
                    tt("dve", gx_h, gx_h, conv_h, ALU.mult, [t_gx, t_conv], [t_gx]); yield
                    if q == 0:
                        init = 0.0; rd = [t_A, t_gx]
                    elif q % 2 == 1:
                        other = LS[(q - 1) % 2]
                        init = other["conv"][:, QW - 1:QW]; rd = [t_A, t_gx, other["t_conv"]]
                    else:
                        init = carry4[:, q - 1:q]; rd = [t_A, t_gx, t_carry[q - 1]]
                    S.op("dve", lambda e, init=init, conv_h=conv_h, A_h=A_h, gx_h=gx_h: e.tensor_tensor_scan(out=conv_h, data0=A_h, data1=gx_h, initial=init,
                                                                           op0=ALU.mult, op1=ALU.add),
                         reads=rd, writes=[t_conv]); yield
                    if q == 1:
                        cp("dve", carry4[:, q:q + 1], conv_h[:, QW - 1:QW], [t_conv], [t_carry[q]])
                    yield
                    yg = ygb[:, c0:c0 + QW]
                    act(ga_h, yg, AF.Gelu_apprx_tanh, [t_ygb, t_ga], [t_ga]); yield
                    tt("dve", mview(n, c0, QW), ga_h, conv_h, ALU.mult, [t_ga, t_conv], [mT[n][q]]); yield

                def inproj_units(col0, dst_fn, dst_tile):
                    hold = {}
                    res = []
                    for t4 in range(4):
                        def mmu(t4=t4):
                            if t4 == 0:
                                hold["sl"] = load_w(Wv[:, :, col0:col0 + 128], DC)
                            sl = hold["sl"]
                            for d in range(DC):
                                mm(PB[t4][:, :], t_PB[t4], wk(sl, d), xTv(d, t4 * 512, 512), d == 0, d == DC - 1, [t_wsl[sl], t_xT[t4]])

                        def evu(t4=t4):
                            evac(dst_fn(t4), PB[t4][:, :], [t_PB[t4]], [dst_tile])
                        res.append((mmu, evu))
                    return res

                def lru_pair(n, qa, qb, units):
                    ga_, gb_ = lru_math(n, qa), lru_math(n, qb)
                    alive = [ga_, gb_]
                    step = 0
                    sched_ = {}
                    for i_, (mmu, evu) in enumerate(units):
                        sched_.setdefault(1 + 4 * i_, []).append(mmu)
                        sched_.setdefault(1 + 4 * i_ + 7, []).append(evu)
                    while alive or any(k_ >= step for k_ in sched_):
                        for g_ in list(alive):
                            try:
                                next(g_)
                            except StopIteration:
                                alive.remove(g_)
                        for f_ in sched_.pop(step, []):
                            f_()
                        step += 1
                        if not alive and not sched_:
                            break

                xr_dst = lambda n: (lambda t4: xrp2[n % 2][:, 3 + t4 * 512:3 + (t4 + 1) * 512])
                yg_dst = lambda n: (lambda t4: ygb2[n % 2][:, t4 * 512:(t4 + 1) * 512])
                inproj_mm(0); inproj_evac(xr_dst(0), [t_xrp2[0]])
                inproj_mm(1024); inproj_evac(yg_dst(0), [t_ygb2[0]])
                for n in range(8):
                    ux = inproj_units((n + 1) * 128, xr_dst(n + 1), t_xrp2[(n + 1) % 2]) if n + 1 < 8 else []
                    uy = inproj_units(1024 + (n + 1) * 128, yg_dst(n + 1), t_ygb2[(n + 1) % 2]) if n + 1 < 8 else []
                    lru_pair(n, 0, 1, ux)
                    lru_pair(n, 2, 3, uy)
                S.barrier()
                car.off = mark
                if stop == 'A1':
                    raise _Stop()

                qT2 = [car.take(SEQ, BF16) for _ in range(2)]; t_q2 = [T("qT0"), T("qT1")]
                kT2 = [car.take(SEQ, BF16) for _ in range(2)]; t_k2 = [T("kT0"), T("kT1")]
                vT2 = [car.take(SEQ, BF16) for _ in range(2)]; t_v2 = [T("vT0"), T("vT1")]
                vaug2 = [car.take(NTB * 130, BF16) for _ in range(2)]; t_va2 = [T("va0"), T("va1")]
                vav2 = [v_.rearrange("p (j c) -> p j c", c=130) for v_ in vaug2]
                NET = 4
                ET = [car.take(512, BF16) for _ in range(NET)]; t_ET = [T("ET%d" % i) for i in range(NET)]
                tAB = [car.take(512, F32), car.take(512, F32)]
                rcp = car.take(512, F32)
                sqb = car.take(512, BF16)
                t_ep = T("ep")
                segidx = {}
                for Tq_ in range(4):
                    for c_ in range(2):
                        segidx[(Tq_, c_)] = len(segidx)
                mx = sm[:, 54:58]
                nrm = sm[:, 58:62]
                nsh2 = [sm[:, 20:22], sm[:, 22:24]]
                t_nsh2 = [T("nsh0"), T("nsh1")]
                t_sm = T("sm")
                for b_ in range(2):
                    S.op("dve", lambda e, b_=b_: e.memset(vav2[b_][:, :, 128:130], 1.0), writes=[t_va2[b_]])
                LA = 1
                BGB = 6
                deferred = []

                def prologue_units(h):
                    hp = h % 2
                    qT, kT, vT = qT2[hp], kT2[hp], vT2[hp]
                    t_q, t_k, t_v, t_va = t_q2[hp], t_k2[hp], t_v2[hp], t_va2[hp]
                    vav = vav2[hp]
                    nsh = nsh2[hp]; t_nsh = t_nsh2[hp]
                    units = []
                    for (col0, dst, t_dst) in ((2048 + h * 128, qT, t_q), (3072 + h * 128, kT, t_k), (4096 + h * 128, vT, t_v)):
                        hold = {}

                        def ld(col0=col0, hold=hold):
                            hold["sl"] = load_w(Wv[:, :, col0:col0 + 128], DC)
                        for t4 in range(4):
                            def u(t4=t4, dst=dst, t_dst=t_dst, hold=hold, ld=ld):
                                if t4 == 0:
                                    ld()
                                sl = hold["sl"]
                                for d in range(DC):
                                    mm(PB[BGB][:, :], t_PB[BGB], wk(sl, d), xTv(d, t4 * 512, 512), d == 0, d == DC - 1,
                                       [t_wsl[sl], t_xT[t4]])
                                cp("dve", dst[:, t4 * 512:(t4 + 1) * 512], PB[BGB][:, :], [t_PB[BGB]], [t_dst])
                            units.append(u)
                    for g in range(4):
                        def u(g=g):
                            for a in range(4):
                                blk = 4 * g + a
                                S.op("pe", lambda e, a=a, blk=blk: e.transpose(PBF[0][:, a * 128:(a + 1) * 128],
                                                                              vT[:, blk * 128:(blk + 1) * 128], identB),
                                     reads=[t_v, t_const], writes=[t_PBF[0]])
                            cp("dve", vav[:, 4 * g:4 * g + 4, 0:128], PBF[0][:, :].rearrange("p (a t) -> p a t", t=128), [t_PBF[0]], [t_va])
                        units.append(u)
                    sq = vT
                    for qi, (src, t_src) in enumerate(((qT, t_q), (kT, t_k))):
                        for c, sel in enumerate((sel0, sel1)):
                            def u(qi=qi, c=c, sel=sel, src=src, t_src=t_src):
                                if c == 0:
                                    act(sq, src, AF.Square, [t_src, t_v], [t_v])
                                for t4 in range(4):
                                    mm(PB[BGB][:, :], t_PB[BGB], sel, sq[:, t4 * 512:(t4 + 1) * 512], True, True, [t_v, t_const])
                                    S.op("dve", lambda e, t4=t4: e.reduce_max(out=mx[:, t4:t4 + 1], in_=PB[BGB][:, :], axis=AX.X),
                                         reads=[t_PB[BGB]], writes=[t_sm])
                                S.op("dve", lambda e, qi=qi, c=c: e.reduce_max(out=nrm[:, qi * 2 + c:qi * 2 + c + 1], in_=mx, axis=AX.X),
                                     reads=[t_sm], writes=[t_sm])
                            units.append(u)

                    def fin():
                        tt("dve", nsh, nrm[:, 0:2], nrm[:, 2:4], ALU.mult, [t_sm], [t_nsh])
                        ts("dve", nsh, nsh, 1e-30, None, ALU.max, None, [t_nsh], [t_nsh])
                        act(nsh, nsh, AF.Ln, [t_nsh], [t_nsh])
                        act(nsh, nsh, AF.Exp, [t_nsh], [t_nsh], scale=0.5)
                        ts("dve", nsh, nsh, -0.125 * 1.02, None, ALU.mult, None, [t_nsh], [t_nsh])
                        ts("dve", nsh, nsh, B8S(h), None, ALU.subtract, None, [t_nsh, tc_], [t_nsh])
                    units.append(fin)
                    return units

                for u_ in prologue_units(0):
                    u_()
                for h in range(8):
                    hp = h % 2
                    qT, kT = qT2[hp], kT2[hp]
                    t_q, t_k, t_va = t_q2[hp], t_k2[hp], t_va2[hp]
                    vav = vav2[hp]
                    nsh = nsh2[hp]; t_nsh = t_nsh2[hp]
                    bg = prologue_units(h + 1) if h + 1 < 8 else []
                    hb_ = HB[:, h * 256:(h + 1) * 256]
                    iters = [(Tq, c, j) for Tq in range(4) for c in range(2) for j in range(4 * Tq + 4)]
                    bg_every = max(1, len(iters) // (len(bg) + 1)) if bg else 0

                    def emit_qk(k):
                        Tq, c, j = iters[k]
                        rs = slice(c * 64, (c + 1) * 64)
                        qlo = max(j, 4 * Tq)
                        ncols = (4 * Tq + 4 - qlo) * 128
                        pss = 4 + (k % 2)
                        es = k % NET
                        near = j >= 4 * Tq - 1
                        mm(PB[pss][:, 0:ncols], t_PB[pss], kT[rs, j * 128:(j + 1) * 128], qT[rs, qlo * 128:(4 * Tq + 4) * 128],
                           True, not near, [t_q, t_k])
                        if j >= 4 * Tq:
                            nb = min(256, ncols)
                            mm(PB[pss][:, 0:nb], t_PB[pss], Jb, hb_[:, 0:nb], False, True, [t_const])
                        elif j == 4 * Tq - 1:
                            mm(PB[pss][:, 0:128], t_PB[pss], Jb, hb_[:, 128:256], False, True, [t_const])
                        act(ET[es][:, 0:ncols], PB[pss][:, 0:ncols], AF.Exp, [t_PB[pss], t_nsh], [t_ET[es]],
                            bias=nsh[:, c:c + 1], scale=0.125)

                    def emit_pv(k):
                        Tq, c, j = iters[k]
                        qlo = max(j, 4 * Tq)
                        ncols = (4 * Tq + 4 - qlo) * 128
                        col0 = (qlo - 4 * Tq) * 128
                        es = k % NET
                        sgi = segidx[(Tq, c)]
                        ob, db = 2 * (sgi % 2), 2 * (sgi % 2) + 1
                        last = (j == 4 * Tq + 3)
                        mm(PB[ob][:, col0:col0 + ncols], t_PB[ob], vav[:, j, 0:128], ET[es][:, 0:ncols], j == 0, last, [t_ET[es], t_va])
                        mm(PB[db][:, col0:col0 + ncols], t_PB[db], onesb, ET[es][:, 0:ncols], j == 0, last, [t_ET[es], t_const])
                        if last:
                            tC = tAB[c]
                            act(rcp, PB[db][:, :], AF.Ln, [t_PB[db], t_ep], [t_ep])
                            act(rcp, rcp, AF.Exp, [t_ep], [t_ep], scale=-1.0)
                            tt("dve", tC, PB[ob][:, :], rcp, ALU.mult, [t_PB[ob], t_ep], [t_ep])
                            if c == 1:
                                epilogue(Tq, db)

                    def epilogue(Tq, db):
                        for dd_ in list(deferred):
                            deferred.remove(dd_)
                            dd_[1]()
                        tA, tB = tAB
                        stt(tA, tB, NEGLAM, tA, ALU.mult, ALU.add, [t_ep, tc_], [t_ep])
                        tt("dve", sqb, tA, tA, ALU.mult, [t_ep], [t_ep])

                        def part2(h=h, Tq=Tq, db=db):
                            mm(PB[db][:, :], t_PB[db], onesb, sqb, True, True, [t_ep, t_const])
                            ts("dve", rcp, PB[db][:, :], 1.0 / 128.0, EPS, ALU.mult, ALU.add, [t_PB[db], t_ep], [t_ep])
                            act(rcp, rcp, AF.Ln, [t_ep], [t_ep])
                            act(rcp, rcp, AF.Exp, [t_ep], [t_ep], scale=-0.5)
                            stt(mview(8 + h, Tq * 512, 512), tA, GCOL, rcp, ALU.mult, ALU.mult, [t_ep, tc_], [mT[8 + h][Tq], t_ep])
                        deferred.append([3, part2])

                    for k in range(len(iters) + LA):
                        if k < len(iters):
                            emit_qk(k)
                        if k - LA >= 0:
                            emit_pv(k - LA)
                        for dd_ in list(deferred):
                            dd_[0] -= 1
                            if dd_[0] <= 0:
                                deferred.remove(dd_)
                                dd_[1]()
                        if bg and k % bg_every == bg_every - 1:
                            bg.pop(0)()
                    while bg:
                        bg.pop(0)()
                for dd_ in deferred:
                    dd_[1]()
                deferred = []
                S.barrier()

                if debug:
                    for e_ in range(DC):
                        dma("pool", DBG[:, e_ * SEQ:(e_ + 1) * SEQ], mview(e_, 0, SEQ), reads=[mT[e_][i] for i in range(4)])
                    S.barrier()
                    continue

                conv_some(10 ** 6)
                car = Carver()
                TT = 512
                y = car.take(DC * TT, F32)
                yv = lambda c: y[:, c * TT:(c + 1) * TT]
                t_y = [T("y%d" % c) for c in range(DC)]
                hb = car.take(DC * TT, BF16)
                hbv = lambda c: hb[:, c * TT:(c + 1) * TT]
                t_hb = [T("hb%d" % c) for c in range(DC)]
                NA = 15
                ab = car.take(16 * TT, BF16)
                abv = lambda j: ab[:, j * TT:(j + 1) * TT]
                t_ab = [T("a%d" % j) for j in range(NA)]
                t_ab16 = T("a16")
                xin0_ = car.take(D, F32)
                if car.off % 2:
                    car.off += 1
                off_x1 = car.off
                xin1_ = car.take(D, F32)
                xin = [xin0_, xin1_]; t_xin = [T("xin0"), T("xin1")]
                wppb = arena[:, off_x1:off_x1 + 2 * D]
                d_wpp = S.new_dma_slot() if s == 0 else d_wpp
                ost = [car.take(D, F32) for _ in range(2)]; t_ost = [T("ost0"), T("ost1")]
                tmpf = [car.take(TT, F32) for _ in range(3)]; t_tmp = [T("tmp%d" % i) for i in range(3)]
                rstd = car.take(TT, F32); t_rstd = T("rstd")
                mean_sb = tmpf[0]; nmr = tmpf[1]; vtmp = tmpf[2]
                pin = car.take(4 * PLE, F32); t_pin = T("pin")
                pbb = car.take(4 * PLE, BF16); t_pbb = T("pbb")
                pT = car.take(2 * TT, BF16); t_pT = T("pT")
                tmi = [0]
                bki = [0]

                def nbank():
                    bki[0] += 1
                    return bki[0] % 6

                def layer_norm(gk, bk, final):
                    y4 = lambda g4: y[:, 4 * g4 * TT:(4 * g4 + 4) * TT]
                    for g4 in range(4):
                        cs = list(range(4 * g4, 4 * g4 + 4))
                        cp("dve", hb[:, 4 * g4 * TT:(4 * g4 + 4) * TT], y4(g4), [t_y[c] for c in cs], [t_hb[c] for c in cs])
                        act(ab[:, 4 * g4 * TT:(4 * g4 + 4) * TT], y4(g4), AF.Square, [t_y[c] for c in cs], [t_ab[min(c, NA - 1)] for c in cs] + [t_ab16])
                    be = nbank()
                    for c in range(DC):
                        mm(PB[6][:, :], t_PB[6], meanm, hbv(c), c == 0, c == DC - 1, [t_const, t_hb[c]])
                    for c in range(DC):
                        mm(PB[be][:, :], t_PB[be], meanm, ab[:, c * TT:(c + 1) * TT], c == 0, c == DC - 1, [t_const, t_ab[min(c, NA - 1)], t_ab16])
                    t_st = t_rstd
                    cp("act", mean_sb, PB[6][:, :], [t_PB[6], t_tmp[0]], [t_st, t_tmp[0]])
                    tt("dve", vtmp, mean_sb, mean_sb, ALU.mult, [t_st, t_tmp[2]], [t_st, t_tmp[2]])
                    tt("dve", vtmp, PB[be][:, :], vtmp, ALU.subtract, [t_PB[be], t_st], [t_st, t_tmp[2]])
                    ts("dve", vtmp, vtmp, 0.0, EPS, ALU.max, ALU.add, [t_st], [t_st, t_tmp[2]])
                    act(rstd, vtmp, AF.Ln, [t_st], [t_st])
                    act(rstd, rstd, AF.Exp, [t_st], [t_st], scale=-0.5)
                    stt(nmr, mean_sb, -1.0, rstd, ALU.mult, ALU.mult, [t_st, t_tmp[1]], [t_st, t_tmp[1]])
                    rs3 = rstd.unsqueeze(1).broadcast_to([128, 4, TT])
                    nm3 = nmr.unsqueeze(1).broadcast_to([128, 4, TT])
                    for g4 in range(4):
                        cs = list(range(4 * g4, 4 * g4 + 4))
                        y43 = y4(g4).rearrange("p (a t) -> p a t", t=TT)
                        tt("dve", y43, y43, rs3, ALU.mult, [t_y[c] for c in cs] + [t_st], [t_y[c] for c in cs])
                        tt("dve", y43, y43, nm3, ALU.add, [t_y[c] for c in cs] + [t_st], [t_y[c] for c in cs])
                        for c in cs:
                            if final:
                                act(yv(c), yv(c), AF.Identity, [t_y[c], tc_], [t_y[c]], bias=LNP(bk, c), scale=LNP(gk, c))
                            else:
                                act(hbv(c), yv(c), AF.Identity, [t_y[c], tc_], [t_hb[c]], bias=LNP(bk, c), scale=LNP(gk, c))
                                act(yv(c), yv(c), AF.Identity, [t_y[c], tc_], [t_y[c]], bias=GA_(bk, c), scale=GA_(gk, c))

                xpre = {}

                def issue_xload(ttok_, b_):
                    if (ttok_, b_) in xpre:
                        return
                    xpre[(ttok_, b_)] = True
                    sl_ = b_ % 2
                    S.op("sp", lambda e, sl_=sl_, b_=b_, ttok_=ttok_, xin=xin: e.dma_start(out=xin[sl_], in_=X[ttok_ + b_ * 128:ttok_ + (b_ + 1) * 128, :]),
                         writes=[t_xin[sl_]], dma=True, dma_slot=d_xin[sl_])

                for t in range(SEQ // TT):
                    ttok = tok0 + t * TT
                    for b in range(4):
                        sl = b % 2
                        issue_xload(ttok, b)
                        for g in range(4):
                            bk = nbank()
                            for a in range(4):
                                c = 4 * g + a
                                S.op("pe", lambda e, bk=bk, a=a, c=c, sl=sl, xin=xin: e.transpose(PB[bk][:, a * 128:(a + 1) * 128],
                                                                                                  xin[sl][:, c * 128:(c + 1) * 128], identF[:]),
                                     reads=[t_xin[sl], t_const], writes=[t_PB[bk]])
                            dst = y[:, 4 * g * TT:(4 * g + 4) * TT].rearrange("p (a t) -> p a t", t=TT)[:, :, b * 128:(b + 1) * 128]
                            src = PB[bk][:, :].rearrange("p (a t) -> p a t", t=128)
                            ts("dve", dst, src, ALPHA, None, ALU.mult, None, [t_PB[bk]], [t_y[4 * g + a_] for a_ in range(4)])
                    for c in range(DC):
                        sl = load_w(Wo[:, :, c * 128:(c + 1) * 128], DC, conv_T["o"])
                        bk = nbank()
                        for e_ in range(DC):
                            mm(PB[bk][:, :], t_PB[bk], wk(sl, e_), mview(e_, t * TT, TT), e_ == 0, e_ == DC - 1, [t_wsl[sl], mT[e_][t]])
                        tt("dve", yv(c), yv(c), PB[bk][:, :], ALU.add, [t_y[c], t_PB[bk]], [t_y[c]])
                    layer_norm(0, 1, False)
                    for (j0, nj) in ((0, 15), (15, 15), (30, 14)):
                        for jj in range(nj):
                            j = j0 + jj
                            slg = load_w(Wg[:, :, j * 128:(j + 1) * 128], DC, conv_T["g"])
                            slu = load_w(Wu[:, :, j * 128:(j + 1) * 128], DC, conv_T["u"])
                            bg = nbank(); bu = nbank()
                            for d in range(DC):
                                mm(PB[bg][:, :], t_PB[bg], wk(slg, d), hbv(d), d == 0, d == DC - 1, [t_wsl[slg], t_hb[d]])
                            for d in range(DC):
                                mm(PB[bu][:, :], t_PB[bu], wk(slu, d), hbv(d), d == 0, d == DC - 1, [t_wsl[slu], t_hb[d]])
                            k_ = tmi[0] % 3; tmi[0] += 1
                            act(tmpf[k_], PB[bg][:, :], AF.Silu, [t_PB[bg]], [t_tmp[k_]])
                            tt("dve", abv(jj), tmpf[k_], PB[bu][:, :], ALU.mult, [t_tmp[k_], t_PB[bu]], [t_ab[jj]])
                        for c in range(DC):
                            sl = load_w(Wd[:, j0:j0 + nj, c * 128:(c + 1) * 128], nj, conv_T["d"])
                            bk = nbank()
                            for jj in range(nj):
                                mm(PB[bk][:, :], t_PB[bk], wk(sl, jj), abv(jj), jj == 0, jj == nj - 1, [t_wsl[sl], t_ab[jj]])
                            tt("dve", yv(c), yv(c), PB[bk][:, :], ALU.add, [t_y[c], t_PB[bk]], [t_y[c]])
                    layer_norm(2, 3, False)
                    for b in range(4):
                        S.op("sp", lambda e, b=b, ttok=ttok, pin=pin: e.dma_start(out=pin[:, b * PLE:(b + 1) * PLE], in_=P_[ttok + b * 128:ttok + (b + 1) * 128, :]),
                             writes=[t_pin], dma=True, dma_slot=d_p)
                    cp("dve", pbb, pin, [t_pin], [t_pbb])
                    for b in range(4):
                        for k2 in range(2):
                            S.op("pe", lambda e, b=b, k2=k2: e.transpose(PBF[0][:, k2 * 128:(k2 + 1) * 128],
                                                                         pbb[:, b * PLE + k2 * 128:b * PLE + (k2 + 1) * 128], identB),
                                 reads=[t_pbb, t_const], writes=[t_PBF[0]])
                        dst = pT.rearrange("p (k t) -> p k t", t=TT)[:, :, b * 128:(b + 1) * 128]
                        evac(dst, PBF[0][:, 0:256].rearrange("p (k t) -> p k t", t=128), [t_PBF[0]], [t_pT])
                    for c in range(DC):
                        if c == 0:
                            S.op("pool", lambda e, wppb=wppb: e.dma_start(out=wppb.rearrange("p (k c) -> p k c", c=D), in_=Wpp[:, :, :]),
                                 writes=[t_xin[1]], dma=True, dma_slot=d_wpp)
                        slg = load_w(Wpg[:, :, c * 128:(c + 1) * 128], DC, conv_T["pg"])
                        bg = nbank(); be = nbank()
                        for d in range(DC):
                            mm(PB[bg][:, :], t_PB[bg], wk(slg, d), hbv(d), d == 0, d == DC - 1, [t_wsl[slg], t_hb[d]])
                        for k2 in range(2):
                            wpp_ = wppb[:, k2 * D + c * 128:k2 * D + (c + 1) * 128]
                            mm(PB[be][:, :], t_PB[be], wpp_, pT[:, k2 * TT:(k2 + 1) * TT], k2 == 0, k2 == 1, [t_xin[1], t_pT])
                        k_ = tmi[0] % 3; tmi[0] += 1
                        act(tmpf[k_], PB[bg][:, :], AF.Sigmoid, [t_PB[bg], tc_], [t_tmp[k_]], bias=LNP(6, c))
                        tt("dve", tmpf[k_], tmpf[k_], PB[be][:, :], ALU.mult, [t_tmp[k_], t_PB[be]], [t_tmp[k_]])
                        tt("dve", yv(c), yv(c), tmpf[k_], ALU.add, [t_y[c], t_tmp[k_]], [t_y[c]])
                    if t + 1 < SEQ // TT:
                        issue_xload(ttok + TT, 0)
                        issue_xload(ttok + TT, 1)
                    layer_norm(4, 5, True)
                    for b in range(4):
                        sl = b % 2
                        for g in range(4):
                            bk = nbank()
                            for a in range(4):
                                c = 4 * g + a
                                S.op("pe", lambda e, bk=bk, a=a, src_=yv(c)[:, b * 128:(b + 1) * 128]: e.transpose(PB[bk][:, a * 128:(a + 1) * 128], src_, identF[:]),
                                     reads=[t_y[c], t_const], writes=[t_PB[bk]])
                            evac(ost[sl][:, g * 512:(g + 1) * 512], PB[bk][:, :], [t_PB[bk]], [t_ost[sl]])
                        S.op("sp", lambda e, sl=sl, b=b, ttok=ttok, ost=ost: e.dma_start(out=OUT[ttok + b * 128:ttok + (b + 1) * 128, :], in_=ost[sl]),
                             reads=[t_ost[sl]], dma=True, dma_slot=d_out[sl])
                S.barrier()
        except _Stop:
            if debug:
                S.barrier()
                for e_ in range(DC):
                    S.op("pool", lambda e, e_=e_: e.dma_start(out=DBG[:, e_ * SEQ:(e_ + 1) * SEQ], in_=mergedT[:, e_ * SEQ:(e_ + 1) * SEQ]),
                         reads=[mT[e_][i] for i in range(4)], dma=True, dma_slot=S.new_dma_slot())
        S.emit()
    return nc


def _host_inputs(inp, ncores=N_CORES, nseq=SPC):
    f = lambda a: np.ascontiguousarray(np.asarray(a, dtype=np.float32))
    cm, oh, mrow = _consts()
    chunkT = lambda v, nch: f(np.asarray(v).reshape(nch, 128).T)
    pvec = np.zeros((128, 176), np.float32)
    cw = np.asarray(inp["conv_w"])[0]
    pvec[:, 0:32] = cw.reshape(4, 8, 128).transpose(2, 1, 0).reshape(128, 32)
    pvec[:, 32:40] = chunkT(inp["conv_b"][0], 8)
    pvec[:, 40:48] = chunkT(inp["lru_ba"][0], 8)
    pvec[:, 48:56] = chunkT(inp["lru_bx"][0], 8)
    pvec[:, 56:64] = chunkT(inp["lru_lambda"][0], 8)
    for k, nm in enumerate(("ln1_g", "ln1_b", "ln2_g", "ln2_b", "ln3_g", "ln3_b", "b_ple_gate")):
        pvec[:, 64 + 16 * k:80 + 16 * k] = chunkT(inp[nm][0], 16)
    brow = np.zeros((1, 640), np.float32)
    brow[0, 0:64] = inp["diff_lq1"][0]
    brow[0, 64:128] = inp["diff_lk1"][0]
    brow[0, 128:192] = inp["diff_lq2"][0]
    brow[0, 192:256] = inp["diff_lk2"][0]
    brow[0, 256:384] = inp["diff_subln_g"][0]
    brow[0, 384:640] = np.asarray(inp["rel_bias"]).reshape(-1)
    lw = np.stack([np.asarray(inp["lru_wa"])[0], np.asarray(inp["lru_wx"])[0]], 0)
    lruw = f(lw.transpose(2, 0, 1, 3).reshape(128, 2 * 8 * 128))
    shared = {
        "w_in": f(inp["w_in"][0]), "w_out": f(inp["w_out"][0]), "w_gate": f(inp["w_ffn_gate"][0]),
        "w_up": f(inp["w_ffn_up"][0]), "w_down": f(inp["w_ffn_down"][0]), "w_pgate": f(inp["w_ple_gate"][0]),
        "w_pproj": f(inp["w_ple_proj"][0]), "lruw": lruw, "pvec": pvec, "brow": brow,
        "relb": f(inp["rel_bias"]), "cmat": cm, "oneh": oh, "mrow": mrow,
    }
    x = np.asarray(inp["x"], dtype=np.float32)
    p = np.asarray(inp["p"], dtype=np.float32)[0]
    maps = []
    for c in range(ncores):
        m = dict(shared)
        m["x"] = f(x[c * nseq:(c + 1) * nseq].reshape(nseq * SEQ, D))
        m["p"] = f(p[c * nseq:(c + 1) * nseq].reshape(nseq * SEQ, PLE))
        maps.append(m)
    return maps


def kernel(**inputs):
    nc = build()
    maps = _host_inputs(inputs)
    res = run_bass_kernel_spmd(nc, maps, core_ids=list(range(N_CORES)))
    outs = [np.asarray(r["out"]).reshape(SPC, SEQ, D) for r in res.results]
    return np.concatenate(outs, axis=0).astype(np.float32)
```
